# Optimizing a Trainium2 kernel written in Bass

```python
import jax, jax.numpy as jnp
from jax import lax
import numpy as np

D_MODEL = 1024
BATCH = 32
SEQ = 2048
DEPTH = 1

CHUNK = 128
GM_HEAD_DIM = 128
GM_WIDTH = D_MODEL // 2
GM_HEADS = GM_WIDTH // GM_HEAD_DIM
ML_HEAD_DIM = 128
ML_WIDTH = D_MODEL - GM_WIDTH
ML_HEADS = ML_WIDTH // ML_HEAD_DIM
D_MIX = GM_WIDTH + ML_WIDTH
ML_CONV = 4
FFN_CONV = 3
D_FF = 2816
EPS = 1e-6
IN_SPLITS = [GM_WIDTH, GM_WIDTH, ML_WIDTH, ML_WIDTH, ML_WIDTH, ML_WIDTH, ML_HEADS, ML_HEADS]
N_IN = sum(IN_SPLITS)

kernel_name = 'hybrid_gmlp_mlstm_convffn_adaln'


def rmsnorm(x, g):
    xf = x.astype(jnp.float32)
    y = xf * lax.rsqrt(jnp.mean(xf * xf, axis=-1, keepdims=True) + EPS)
    return (y * g.astype(jnp.float32)).astype(x.dtype)


def modulate(h, shift, scale):
    return h * (1 + scale[:, None, :]) + shift[:, None, :]


def causal_dwconv(x, w, b):
    K, C = w.shape
    y = lax.conv_general_dilated(
        x, w[:, None, :].astype(x.dtype), window_strides=(1,), padding=[(K - 1, 0)],
        dimension_numbers=('NWC', 'WIO', 'NWC'), feature_group_count=C)
    return y + b.astype(x.dtype)


def gmlp_mix(u, v, vnorm_g, w_s, b_s):
    B, S, _ = u.shape
    nc = S // CHUNK
    u = jax.nn.gelu(u, approximate=False)
    v = jax.nn.gelu(v, approximate=False)
    v = rmsnorm(v.reshape(B, S, GM_HEADS, GM_HEAD_DIM), vnorm_g.reshape(GM_HEADS, GM_HEAD_DIM))
    v = v.reshape(B, nc, CHUNK, GM_HEADS, GM_HEAD_DIM)
    mask = jnp.tril(jnp.ones((CHUNK, CHUNK), dtype=bool))
    w = jnp.where(mask[None], w_s, jnp.zeros_like(w_s))
    mixed = jnp.einsum('hts,bcshe->bcthe', w, v) + jnp.transpose(b_s)[None, None, :, :, None]
    return u * mixed.reshape(B, S, GM_WIDTH)


def mlstm_cell(q, k, v, i_pre, f_pre):
    B, S, H, dh = q.shape
    nc = S // CHUNK
    to_chunks = lambda t: jnp.transpose(t.reshape(B, nc, CHUNK, H, dh), (0, 3, 1, 2, 4))
    gate_chunks = lambda t: jnp.transpose(t.reshape(B, nc, CHUNK, H), (0, 3, 1, 2))
    q = to_chunks(q)
    k = to_chunks(k) * (dh ** -0.5)
    v = to_chunks(v)
    ii = gate_chunks(i_pre)
    logf = jax.nn.log_sigmoid(gate_chunks(f_pre))
    b = jnp.cumsum(logf, axis=-1)
    bL = b[..., -1]
    a = bL[..., None] - b + ii
    a_max = jnp.max(a, axis=-1)
    wgt = jnp.exp(a - a_max[..., None])
    S_c = jnp.einsum('bhcs,bhcse,bhcsd->bhced', wgt, v, k)
    n_c = jnp.einsum('bhcs,bhcsd->bhcd', wgt, k)

    def step(carry, inp):
        C, n, m = carry
        Sc, nc_, bl, am = inp
        m_new = jnp.maximum(bl + m, am)
        decay = jnp.exp(bl + m - m_new)
        inw = jnp.exp(am - m_new)
        C_new = decay[..., None, None] * C + inw[..., None, None] * Sc
        n_new = decay[..., None] * n + inw[..., None] * nc_
        return (C_new, n_new, m_new), (C, n, m)

    init = (jnp.zeros((B, H, dh, dh), jnp.float32), jnp.zeros((B, H, dh), jnp.float32),
            jnp.zeros((B, H), jnp.float32))
    xs = (jnp.moveaxis(S_c, 2, 0), jnp.moveaxis(n_c, 2, 0), jnp.moveaxis(bL, 2, 0), jnp.moveaxis(a_max, 2, 0))
    _, (C_prev, n_prev, m_prev) = lax.scan(step, init, xs)
    C_prev = jnp.moveaxis(C_prev, 0, 2)
    n_prev = jnp.moveaxis(n_prev, 0, 2)
    m_prev = jnp.moveaxis(m_prev, 0, 2)

    mask = jnp.tril(jnp.ones((CHUNK, CHUNK), dtype=bool))
    D = b[..., :, None] - b[..., None, :] + ii[..., None, :]
    D = jnp.where(mask, D, -jnp.inf)
    m_intra = jnp.max(D, axis=-1)
    inter = b + m_prev[..., None]
    m = jnp.maximum(inter, m_intra)
    P = jnp.exp(D - m[..., None])
    scores = jnp.einsum('bhcjd,bhcsd->bhcjs', q, k) * P
    inter_w = jnp.exp(inter - m)
    num = (jnp.einsum('bhcjs,bhcse->bhcje', scores, v)
           + inter_w[..., None] * jnp.einsum('bhcjd,bhced->bhcje', q, C_prev))
    den = jnp.sum(scores, axis=-1) + inter_w * jnp.einsum('bhcjd,bhcd->bhcj', q, n_prev)
    h = num / jnp.maximum(jnp.abs(den), jnp.exp(-m))[..., None]
    return jnp.transpose(h, (0, 2, 3, 1, 4)).reshape(B, S, H, dh)


def mlstm_mix(q_pre, k_pre, v, o, i_pre, f_pre, conv_w, conv_b, i_b, f_b, hnorm_g):
    B, S, _ = q_pre.shape
    qk = jax.nn.silu(causal_dwconv(jnp.concatenate([q_pre, k_pre], axis=-1), conv_w, conv_b))
    q, k = jnp.split(qk, 2, axis=-1)
    shp = (B, S, ML_HEADS, ML_HEAD_DIM)
    h = mlstm_cell(q.reshape(shp).astype(jnp.float32), k.reshape(shp).astype(jnp.float32),
                   v.reshape(shp).astype(jnp.float32),
                   (i_pre + i_b).astype(jnp.float32), (f_pre + f_b).astype(jnp.float32))
    h = rmsnorm(h, hnorm_g.reshape(ML_HEADS, ML_HEAD_DIM)).reshape(B, S, ML_WIDTH)
    return (jax.nn.sigmoid(o.astype(jnp.float32)) * h).astype(q_pre.dtype)


def conv_ffn(h, w_up, conv_w, conv_b, w_down):
    g, u = jnp.split(h @ w_up, 2, axis=-1)
    g = jax.nn.gelu(causal_dwconv(g, conv_w, conv_b), approximate=False)
    return (g * u) @ w_down


def setup_inputs(seed: int = 0) -> dict:
    key = jax.random.key(seed)
    ks = jax.random.split(key, 24)
    f32 = jnp.float32
    nrm = lambda k, shape, fan_in: jax.random.normal(k, shape, f32) * (fan_in ** -0.5)
    rnd = lambda k, shape: jax.random.normal(k, shape, f32)
    return {
        'x': rnd(ks[0], (BATCH, SEQ, D_MODEL)),
        'c': rnd(ks[1], (BATCH, D_MODEL)),
        'ada_w': 0.5 * nrm(ks[2], (DEPTH, D_MODEL, 6 * D_MODEL), D_MODEL),
        'ada_b': 0.02 * rnd(ks[3], (DEPTH, 6 * D_MODEL)),
        'norm1_g': 1.0 + 0.02 * rnd(ks[4], (DEPTH, D_MODEL)),
        'w_in': nrm(ks[5], (DEPTH, D_MODEL, N_IN), D_MODEL),
        'gm_vnorm_g': 1.0 + 0.02 * rnd(ks[6], (DEPTH, GM_WIDTH)),
        'gm_spatial_w': nrm(ks[7], (DEPTH, GM_HEADS, CHUNK, CHUNK), CHUNK),
        'gm_spatial_b': 1.0 + 0.1 * rnd(ks[8], (DEPTH, GM_HEADS, CHUNK)),
        'ml_conv_w': nrm(ks[9], (DEPTH, ML_CONV, 2 * ML_WIDTH), ML_CONV),
        'ml_conv_b': 0.02 * rnd(ks[10], (DEPTH, 2 * ML_WIDTH)),
        'ml_i_b': 0.1 * rnd(ks[11], (DEPTH, ML_HEADS)),
        'ml_f_b': jnp.linspace(3.0, 6.0, ML_HEADS, dtype=f32)[None, :] + 0.1 * rnd(ks[12], (DEPTH, ML_HEADS)),
        'ml_hnorm_g': 1.0 + 0.02 * rnd(ks[13], (DEPTH, ML_WIDTH)),
        'w_out': nrm(ks[14], (DEPTH, D_MIX, D_MODEL), D_MIX),
        'norm2_g': 1.0 + 0.02 * rnd(ks[15], (DEPTH, D_MODEL)),
        'ffn_w_up': nrm(ks[16], (DEPTH, D_MODEL, 2 * D_FF), D_MODEL),
        'ffn_conv_w': nrm(ks[17], (DEPTH, FFN_CONV, D_FF), FFN_CONV),
        'ffn_conv_b': 0.02 * rnd(ks[18], (DEPTH, D_FF)),
        'ffn_w_down': nrm(ks[19], (DEPTH, D_FF, D_MODEL), D_FF),
        'final_ada_w': 0.5 * nrm(ks[20], (D_MODEL, 2 * D_MODEL), D_MODEL),
        'final_ada_b': 0.02 * rnd(ks[21], (2 * D_MODEL,)),
        'final_g': 1.0 + 0.02 * rnd(ks[22], (D_MODEL,)),
    }


def reference(x, c, ada_w, ada_b, norm1_g, w_in, gm_vnorm_g, gm_spatial_w, gm_spatial_b,
              ml_conv_w, ml_conv_b, ml_i_b, ml_f_b, ml_hnorm_g, w_out, norm2_g,
              ffn_w_up, ffn_conv_w, ffn_conv_b, ffn_w_down, final_ada_w, final_ada_b, final_g):
    split_idx = np.cumsum(IN_SPLITS)[:-1].tolist()
    c_act = jax.nn.silu(c)
    for l in range(DEPTH):
        mod = c_act @ ada_w[l] + ada_b[l]
        sh1, sc1, g1, sh2, sc2, g2 = jnp.split(mod, 6, axis=-1)
        h = modulate(rmsnorm(x, norm1_g[l]), sh1, sc1)
        gu, gv, mq, mk, mv, mo, mi, mf = jnp.split(h @ w_in[l], split_idx, axis=-1)
        y_gm = gmlp_mix(gu, gv, gm_vnorm_g[l], gm_spatial_w[l], gm_spatial_b[l])
        y_ml = mlstm_mix(mq, mk, mv, mo, mi, mf, ml_conv_w[l], ml_conv_b[l],
                         ml_i_b[l], ml_f_b[l], ml_hnorm_g[l])
        y = jnp.concatenate([y_gm, y_ml], axis=-1) @ w_out[l]
        x = x + g1[:, None, :] * y
        h = modulate(rmsnorm(x, norm2_g[l]), sh2, sc2)
        x = x + g2[:, None, :] * conv_ffn(h, ffn_w_up[l], ffn_conv_w[l], ffn_conv_b[l], ffn_w_down[l])
    fmod = c_act @ final_ada_w + final_ada_b
    f_sh, f_sc = jnp.split(fmod, 2, axis=-1)
    return modulate(rmsnorm(x, final_g), f_sh, f_sc)
```

```python
import math
from contextlib import ExitStack

import numpy as np
import concourse.bass as bass
import concourse.mybir as mybir
from concourse.bass_utils import run_bass_kernel_spmd

F32 = mybir.dt.float32
BF16 = mybir.dt.bfloat16
AF = mybir.ActivationFunctionType
ALU = mybir.AluOpType
AX = mybir.AxisListType

D = 1024
S = 2048
NCORE = 8
DFF = 2816
NJ = 22
NIN = 3080
G = 512
EPS = 1e-6
LN_SQRT_DH = 0.5 * math.log(128.0)


class Prog:
    def __init__(self):
        self.ops = []
        self.last_w = {}
        self.readers = {}

    def add(self, eng, fn, reads=(), writes=(), dma=None, nws=False):
        i = len(self.ops)
        deps = {}
        wset = set(writes)
        for k in reads:
            if k in wset:
                continue
            w = self.last_w.get(k)
            if w is not None:
                deps[w] = True
        for k in wset:
            w = self.last_w.get(k)
            if w is not None:
                deps[w] = True
            for r in self.readers.get(k, ()):
                if r not in deps:
                    deps[r] = False
        for k in wset:
            self.last_w[k] = i
            self.readers[k] = []
        for k in reads:
            if k in wset:
                continue
            lst = self.readers.setdefault(k, [])
            if dma is None:
                lst[:] = [r for r in lst if not (self.ops[r]['dma'] is None and self.ops[r]['eng'] == eng)]
            lst.append(i)
        self.ops.append(dict(eng=eng, fn=fn, deps=deps, dma=dma, nws=nws))
        return i

    def emit(self, nc, es):
        engs = {'pe': nc.tensor, 'act': nc.scalar, 'dve': nc.vector, 'pool': nc.gpsimd, 'sp': nc.sync}
        ops = self.ops
        def needs_wait(op, d, hard):
            dop = ops[d]
            if dop['dma'] is not None or op['dma'] is not None and dop['eng'] != op['eng']:
                return True
            if dop['eng'] != op['eng']:
                return True
            if op['eng'] == 'pe' or op['nws']:
                return False
            return hard
        signal = set()
        for op in ops:
            for d, hard in op['deps'].items():
                if needs_wait(op, d, hard):
                    signal.add(d)
        sem_e = {e: es.enter_context(nc.semaphore("se_" + e)) for e in engs}
        dma_streams = sorted({op['dma'] for op in ops if op['dma'] is not None}, key=str)
        sem_d = {s: es.enter_context(nc.semaphore("sd_%d" % n)) for n, s in enumerate(dma_streams)}
        cnt_e = {e: 0 for e in engs}
        cnt_d = {s: 0 for s in dma_streams}
        token = {}
        seen = {e: {} for e in engs}
        nwait = 0
        for i, op in enumerate(ops):
            eng = op['eng']
            e = engs[eng]
            need = {}
            for d, hard in op['deps'].items():
                if not needs_wait(op, d, hard):
                    continue
                sem, val, skey = token[d]
                if need.get(skey, (None, 0))[1] < val:
                    need[skey] = (sem, val)
            for skey, (sem, val) in need.items():
                if seen[eng].get(skey, 0) >= val:
                    continue
                e.wait_ge(sem, val)
                seen[eng][skey] = val
                nwait += 1
            ins = op['fn'](e)
            if op['dma'] is not None:
                s = op['dma']
                cnt_d[s] += 16
                ins.then_inc(sem_d[s], 16)
                token[i] = (sem_d[s], cnt_d[s], ('d', s))
            elif i in signal:
                cnt_e[eng] += 1
                ins.then_inc(sem_e[eng], 1)
                token[i] = (sem_e[eng], cnt_e[eng], ('e', eng))
                seen[eng][('e', eng)] = max(seen[eng].get(('e', eng), 0), 0)
        for s in dma_streams:
            if cnt_d[s] > 0 and str(s).startswith("('out'"):
                nc.sync.wait_ge(sem_d[s], cnt_d[s])
        return nwait


def build(NSEQ=4, TPS=4, debug=False):
    nc = bass.Bass("TRN2", target_bir_lowering=False)
    NT = NSEQ * TPS
    NTOK = NSEQ * S

    def din(name, shape, dt=F32):
        return nc.dram_tensor(name, shape, dt, kind="ExternalInput").ap()

    x_d = din("x", [NTOK, D])
    c_d = din("c", [NSEQ, D])
    ada_w = din("ada_w", [D, 6 * D])
    ada_b = din("ada_b", [6 * D])
    norm1_g = din("norm1_g", [D])
    w_in = din("w_in", [D, NIN])
    vnorm_g = din("gm_vnorm_g", [512])
    sp_w = din("gm_spatial_w", [4, 128, 128])
    sp_b = din("gm_spatial_b", [512])
    mlcw = din("ml_conv_w", [4, 1024])
    mlcb = din("ml_conv_b", [1024])
    ml_ib = din("ml_i_b", [4, 1])
    ml_fb = din("ml_f_b", [4, 1])
    hnorm_g = din("ml_hnorm_g", [512])
    w_out = din("w_out", [D, D])
    norm2_g = din("norm2_g", [D])
    w_up = din("ffn_w_up", [D, 2 * DFF])
    fcw = din("ffn_conv_w", [3, DFF])
    fcb = din("ffn_conv_b", [DFF])
    w_dn = din("ffn_w_down", [DFF, D])
    fada_w = din("final_ada_w", [D, 2 * D])
    fada_b = din("final_ada_b", [2 * D])
    final_g = din("final_g", [D])
    out_d = nc.dram_tensor("out", [NTOK, D], F32, kind="ExternalOutput").ap()

    def dscr(name, shape, dt):
        return nc.dram_tensor(name, shape, dt, kind="Internal").ap()

    win_s = dscr("win_s", [D, NIN], BF16)
    wout_s = dscr("wout_s", [D, D], BF16)
    wup_s = dscr("wup_s", [D, 2 * DFF], BF16)
    wdn_s = dscr("wdn_s", [DFF, D], BF16)
    mod_s = dscr("mod_s", [NSEQ, 8 * D], F32)

    P = Prog()
    es = ExitStack()

    def sb(name, shape, dt=F32):
        return es.enter_context(nc.sbuf_tensor(name, shape, dt))

    def ps(name, shape, dt=F32):
        return es.enter_context(nc.psum_tensor(name, shape, dt))

    NWS = 3
    wslot = [sb("wslot%d" % i, [128, 4096], BF16) for i in range(NWS)]
    NDS = 3
    dslot = [sb("dslot%d" % i, [128, 2048], BF16) for i in range(NDS)]
    xs = [sb("xs%d" % i, [128, D]) for i in range(8)]
    hT = sb("hT", [128, 8 * G], BF16)
    uT = sb("uT", [128, 4 * G], BF16)
    qT = sb("qT", [128, 4 * G], BF16)
    kT = sb("kT", [128, 4 * G], BF16)
    cst = [sb("cst%d" % i, [128, 3 + G]) for i in range(2)]
    cacc = [sb("cacc%d" % i, [128, G]) for i in range(2)]
    qkhalo = sb("qkhalo", [128, 8 * 3])
    vg = sb("vg", [128, 4 * 512], BF16)
    vm = sb("vm", [128, 4 * 512], BF16)
    og = sb("og", [128, 4 * 512], BF16)
    ymT = sb("ymT", [128, 8 * G], BF16)
    mj = sb("mj", [128, NJ * G], BF16)
    gst = [sb("gst%d" % i, [128, 2 + G]) for i in range(2)]
    ghalo = sb("ghalo", [128, NJ * 2])
    tA = [sb("tA%d" % i, [128, 512]) for i in range(2)]
    tB = [sb("tB%d" % i, [128, 512]) for i in range(2)]
    xn = [sb("xn%d" % i, [128, D], BF16) for i in range(2)]
    sm = [sb("sm%d" % i, [128, 64]) for i in range(2)]
    G1bc = sb("G1bc", [128, D]); G2bc = sb("G2bc", [128, D]); FSbc = sb("FSbc", [128, D]); FHbc = sb("FHbc", [128, D])
    vnbc = sb("vnbc", [128, 512]); hnbc = sb("hnbc", [128, 512]); bsbc = sb("bsbc", [128, 512])
    modT = sb("modT", [128, 8 * 8 * NSEQ])
    n1g = sb("n1g", [128, 8]); n2g = sb("n2g", [128, 8])
    gm1 = sb("gm1", [128, 8 * NSEQ]); gm2 = sb("gm2", [128, 8 * NSEQ])
    cw = sb("cw", [128, 4 * 8]); cb = sb("cb", [128, 8])
    fw = sb("fw", [128, 3 * NJ]); fb = sb("fb", [128, NJ])
    wg32 = sb("wg32", [128, 64]); wgb = sb("wgb", [128, 64], BF16)
    cT = sb("cT", [128, 8 * NSEQ])
    modrow = [sb("modrow%d" % i, [NSEQ, 256]) for i in range(2)]; biasrow = [sb("biasrow%d" % i, [NSEQ, 256]) for i in range(2)]
    ident32 = sb("ident32", [128, 128]); identb = sb("identb", [128, 128], BF16)
    maskb = sb("maskb", [128, 512], BF16)
    ones4 = sb("ones4", [4, 128]); mask4 = sb("mask4", [4, 4]); zrow = sb("zrow", [4, 128])
    onescol = sb("onescol", [128, 1], BF16); negh = sb("negh", [128, 4]); onesrow = sb("onesrow", [1, 128], BF16)
    WsT = sb("WsT", [128, 512], BF16)
    ibt = sb("ibt", [4, 1]); fbt = sb("fbt", [4, 1]); nfbt = sb("nfbt", [4, 1])
    irow = sb("irow", [4, G]); lfrow = sb("lfrow", [4, G])
    nbrow = sb("nbrow", [4, G]); arow = sb("arow", [4, G]); cmrow = sb("cmrow", [4, G]); e1row = arow; Mrow = cmrow
    mprev = [sb("mprev%d" % i, [4, 1]) for i in range(2)]
    CT = sb("CT", [128, 512]); nst = sb("nst", [128, 4]); CTb = sb("CTb", [128, 512], BF16)
    pt = sb("pt", [128, 512]); STb = sb("STb", [128, 512], BF16)
    vw = sb("vw", [128, 512], BF16); wcol = sb("wcol", [128, 4], BF16); ktok = sb("ktok", [128, 512], BF16)
    yml = sb("yml", [128, 512], BF16)
    aS4 = sb("aS4", [128, 16]); sce4 = sb("sce4", [128, 64]); ss2 = sb("ss2", [128, 8]); nbv = sb("nbv", [128, 16], BF16)

    mb32 = tA[0]; wsp32 = tA[1]; wspb = yml
    ptv = mj[:, 0:4096].bitcast(F32)
    yml2 = mj[:, 4096:4608]
    STb2 = mj[:, 4608:5120]
    r6s = [mj[0:4, 7680:9216].bitcast(F32), mj[0:4, 5120:6656].bitcast(F32)]
    t1s = [tA[1][:], mj[:, 7680:8704].bitcast(F32)]
    t2s = [tB[1][:], mj[:, 8704:9728].bitcast(F32)]
    T1K = [[('tA', 1)], [('mj', 15), ('mj', 16)]]
    T2K = [[('tB', 1)], [('mj', 17), ('mj', 18)]]
    CTbv = [CTb[:], mj[:, 9728:10240], mj[:, 10240:10752], mj[:, 10752:11264]]
    CTBK = ['CTb', ('mj', 19), ('mj', 20), ('mj', 21)]
    bds = [mj[0:4, 9216:10240].bitcast(F32), mj[0:4, 6656:7680].bitcast(F32)]
    R6K = [[('mj', 15), ('mj', 16), ('mj', 17)], [('mj', 10), ('mj', 11), ('mj', 12)]]
    BDK = [[('mj', 18), ('mj', 19)], [('mj', 13), ('mj', 14)]]
    Fb = [ps("F%d" % i, [128, 512]) for i in range(6)]
    Tb = [ps("T%d" % i, [128, 1024], BF16) for i in range(2)]

    def FK(i):
        return ('F', i)

    def TK(i):
        return ('T', i)

    def v3(ap, a):
        return ap.rearrange("p (a b) -> p a b", a=a)

    def bc(ap, shape):
        return ap.to_broadcast(shape)

    def act(out, in_, func, reads, writes, bias=None, scale=None, accum=None, nws=False):
        kw = {}
        if bias is not None:
            kw['bias'] = bias
        if scale is not None:
            kw['scale'] = scale
        if accum is not None:
            kw['accum_out'] = accum
        P.add('act', lambda e: e.activation(out=out, in_=in_, func=func, **kw), reads, writes, nws=nws)

    def tt(eng, out, in0, in1, op, reads, writes):
        P.add(eng, lambda e: e.tensor_tensor(out=out, in0=in0, in1=in1, op=op), reads, writes)

    def ts(eng, out, in0, s1, s2, op0, op1, reads, writes):
        if s2 is None:
            P.add(eng, lambda e: e.tensor_scalar(out=out, in0=in0, scalar1=s1, scalar2=None, op0=op0), reads, writes)
        else:
            P.add(eng, lambda e: e.tensor_scalar(out=out, in0=in0, scalar1=s1, scalar2=s2, op0=op0, op1=op1), reads, writes)

    def stt(out, in0, scalar, in1, op0, op1, reads, writes):
        P.add('dve', lambda e: e.scalar_tensor_tensor(out=out, in0=in0, scalar=scalar, in1=in1, op0=op0, op1=op1), reads, writes)

    def cp(eng, out, in_, reads, writes):
        if eng == 'act':
            P.add('act', lambda e: e.copy(out=out, in_=in_), reads, writes)
        else:
            P.add(eng, lambda e: e.tensor_copy(out=out, in_=in_), reads, writes)

    def mm(out, lhsT, rhs, start, stop, reads, writes):
        P.add('pe', lambda e: e.matmul(out, lhsT=lhsT, rhs=rhs, start=start, stop=stop, skip_group_check=True), reads, writes)

    def tr(out, in_, reads, writes):
        P.add('pe', lambda e: e.transpose(out, in_, identb[:]), reads + ['identb'], writes)

    def dma(q, out, in_, reads, writes, stream):
        P.add(q, lambda e: e.dma_start(out=out, in_=in_), reads, writes, dma=stream)

    def memset(eng, ap, val, writes):
        P.add(eng, lambda e: e.memset(ap, val), [], writes)

    def pw(out, in_, reads, writes):
        n = out.shape[-1]
        P.add('pool', lambda e: e.tensor_tensor(out=out, in0=in_, in1=negh[0:out.shape[0], 0:n], op=ALU.pow), reads + ['negh'], writes)

    KEY_WIN = [('win_s', i) for i in range(4)]
    KEY_WOUT = [('wout_s', i) for i in range(4)]
    KEY_WUP = [('wup_s', i) for i in range(4)]
    KEY_WDN = [('wdn_s', i) for i in range(4)]

    memset('dve', ident32[:], 0.0, ['ident32'])
    P.add('pool', lambda e: e.affine_select(out=ident32[:], in_=ident32[:], pattern=[[-1, 128]], compare_op=ALU.not_equal,
                                            fill=1.0, base=0, channel_multiplier=1), [], ['ident32'])
    cp('dve', identb[:], ident32[:], ['ident32'], ['identb'])
    memset('dve', mb32[:], 0.0, [('tA', 0)])
    P.add('pool', lambda e: e.affine_select(out=v3(mb32[:], 4), in_=v3(mb32[:], 4), pattern=[[0, 4], [1, 128]], compare_op=ALU.is_ge,
                                            fill=-30000.0, base=0, channel_multiplier=-1), [], [('tA', 0)])
    cp('dve', maskb[:], mb32[:], [('tA', 0)], ['maskb'])
    memset('dve', ones4[:], 1.0, ['ones4'])
    memset('dve', zrow[:], 0.0, ['zrow'])
    memset('dve', mask4[:], 0.0, ['mask4'])
    P.add('pool', lambda e: e.affine_select(out=mask4[:], in_=mask4[:], pattern=[[-1, 4]], compare_op=ALU.not_equal,
                                            fill=1.0, base=0, channel_multiplier=1), [], ['mask4'])
    memset('dve', onescol[:], 1.0, ['onescol'])
    memset('dve', negh[:], -0.5, ['negh'])
    dma('sp', v3(wsp32[:], 4), sp_w.rearrange("h t s -> t h s"), [], [('tA', 1)], ('misc', 0))
    P.add('pool', lambda e: e.affine_select(out=v3(wsp32[:], 4), in_=v3(wsp32[:], 4), pattern=[[0, 4], [-1, 128]], compare_op=ALU.is_ge,
                                            fill=0.0, base=0, channel_multiplier=1), [], [('tA', 1)])
    cp('dve', wspb[:], wsp32[:], [('tA', 1)], ['yml'])
    for h in range(4):
        tr(Tb[0][:, h * 128:(h + 1) * 128], wspb[:, h * 128:(h + 1) * 128], ['yml'], [TK(0)])
    cp('dve', WsT[:], Tb[0][:, 0:512], [], [TK(0), 'WsT'])
    dma('sp', vnbc[:], vnorm_g.partition_broadcast(128), [], ['vnbc'], ('misc', 1))
    dma('sp', hnbc[:], hnorm_g.partition_broadcast(128), [], ['hnbc'], ('misc', 2))
    ts('dve', hnbc[:], hnbc[:], math.sqrt(128.0), None, ALU.mult, None, [], ['hnbc'])
    bsh = bsbc[:].bitcast(BF16)
    bs_hi = bsh[0:1, 0:512]; bs_lo = bsh[0:1, 512:1024]
    dma('sp', tA[0][0:1, :], sp_b.rearrange("(o n) -> o n", o=1), [], [('tA', 0)], ('misc', 3))
    cp('dve', bs_hi, tA[0][0:1, :], [('tA', 0)], ['bsbc'])
    tt('dve', tA[1][0:1, :], tA[0][0:1, :], bs_hi, ALU.subtract, [('tA', 0), 'bsbc'], [('tA', 1)])
    cp('dve', bs_lo, tA[1][0:1, :], [('tA', 1)], ['bsbc'])
    memset('dve', onesrow[:], 1.0, ['onesrow'])
    dma('sp', n1g[:], norm1_g.rearrange("(k p) -> p k", p=128), [], ['n1g'], ('misc', 5))
    dma('sp', n2g[:], norm2_g.rearrange("(k p) -> p k", p=128), [], ['n2g'], ('misc', 6))
    dma('sp', v3(cw[:], 4), mlcw.rearrange("t (k p) -> p t k", p=128), [], ['cw'], ('misc', 7))
    dma('sp', cb[:], mlcb.rearrange("(k p) -> p k", p=128), [], ['cb'], ('misc', 8))
    dma('sp', v3(fw[:], 3), fcw.rearrange("t (k p) -> p t k", p=128), [], ['fw'], ('misc', 9))
    dma('sp', fb[:], fcb.rearrange("(k p) -> p k", p=128), [], ['fb'], ('misc', 10))
    dma('sp', ibt[:], ml_ib, [], ['ibt'], ('misc', 11))
    dma('sp', fbt[:], ml_fb, [], ['fbt'], ('misc', 12))
    ts('dve', nfbt[:], fbt[:], -1.0, None, ALU.mult, None, ['fbt'], ['nfbt'])
    dma('sp', v3(wg32[:], 8), w_in[:, 3072:3080].rearrange("(k p) c -> p k c", p=128), [], ['wg32'], ('misc', 13))
    cp('dve', wgb[:], wg32[:], ['wg32'], ['wgb'])
    for b_ in range(NSEQ):
        dma('sp', v3(cT[:], 8)[:, :, b_], c_d[b_, :].rearrange("(k p) -> p k", p=128), [], [('cTl', b_)], ('cT', b_))
    act(cT[:], cT[:], AF.Silu, [('cTl', b_) for b_ in range(NSEQ)], ['cT'])
    ngrp = 32
    for g in range(ngrp):
        sl = g % NWS
        w32 = wslot[sl][:].bitcast(F32).rearrange("p (k c) -> p k c", k=8)
        if g < 24:
            src = ada_w[:, g * 256:(g + 1) * 256]
        else:
            src = fada_w[:, (g - 24) * 256:(g - 23) * 256]
        dma('sp', w32, src.rearrange("(k p) c -> p k c", p=128), [], [('ws', sl), ('ws2', sl)] + (['adaload'] if g == ngrp - 1 else []), ('ws', sl))
        bank = 4 + (g % 2)
        for k in range(8):
            mm(Fb[bank][0:NSEQ, 0:256], v3(cT[:], 8)[:, k, :], w32[:, k, :], k == 0, k == 7, ['cT', ('ws', sl), ('ws2', sl)], [FK(bank)])
        bsrc = ada_b[g * 256:(g + 1) * 256] if g < 24 else fada_b[(g - 24) * 256:(g - 23) * 256]
        dma('act', biasrow[g % 2][:], bsrc.partition_broadcast(NSEQ), [], [('biasrow', g % 2)], ('brow', g % 2))
        tt('dve', modrow[g % 2][:], Fb[bank][0:NSEQ, 0:256], biasrow[g % 2][:], ALU.add,
           [('biasrow', g % 2)], [FK(bank), ('modrow', g % 2)])
        dma('act', mod_s[:, g * 256:(g + 1) * 256], modrow[g % 2][:], [('modrow', g % 2)], [('mod_s', g)], ('mrow', g % 2))
        v_ = g // 4
        if v_ in (0, 1, 3, 4):
            for kk in range(2):
                k_ = (g % 4) * 2 + kk
                col = (v_ * 8 + k_) * NSEQ
                mm(Fb[3][:, col:col + NSEQ], modrow[g % 2][:, kk * 128:(kk + 1) * 128], mask4[0:NSEQ, 0:NSEQ], True, True,
                   [('modrow', g % 2), 'mask4'], [FK(3)])
    modT4 = modT[:].rearrange("p (v k b) -> p v k b", v=8, k=8)
    cp('dve', modT[:, 0:5 * 8 * NSEQ], Fb[3][:, 0:5 * 8 * NSEQ], [], [FK(3)] + [('modT', v) for v in range(8)])
    for (dst, src, nm, rows, rd) in ((win_s, w_in, 'win_s', D, ['adaload']), (wout_s, w_out, 'wout_s', D, ['adaload']),
                                     (wup_s, w_up, 'wup_s', D, KEY_WIN + KEY_WOUT), (wdn_s, w_dn, 'wdn_s', DFF, KEY_WIN + KEY_WOUT)):
        nsp = 4 if rows % 4 == 0 else 2
        rr = rows // nsp
        for i in range(nsp):
            dma('pool', dst[i * rr:(i + 1) * rr, :], src[i * rr:(i + 1) * rr, :], rd, [(nm, i)], ('cast', nm, i))
    for (gm, ng, vi, nm, ngn) in ((gm1, n1g, 1, 'gm1', 'n1g'), (gm2, n2g, 4, 'gm2', 'n2g')):
        ts('dve', v3(gm[:], 8), modT4[:, vi, :, :], 1.0, None, ALU.add, None, [('modT', vi)], [nm])
        tt('dve', v3(gm[:], 8), v3(gm[:], 8), bc(ng[:, :, None], [128, 8, NSEQ]), ALU.mult, [ngn], [nm])
    SH1, SH2 = 0, 3
    G1V, G2V, FSHV, FSCV = 2, 5, 6, 7

    items = []
    for t in range(NT):
        for g in range(3):
            items.append(('in_f', g))
        for g in range(3):
            items.append(('in_t', g))
        items.append(('out', 0))
        items.append(('out', 1))
        for jj in range(11):
            items.append(('up', jj))
    issued = [0]
    slot_of = {}

    def issue_item(n):
        kind, g = items[n]
        sl = n % NWS
        slot_of[n] = sl
        w3 = v3(wslot[sl][:], 8)
        key = ('ws', sl)
        key2 = ('ws2', sl)
        st = ('ws', sl)
        if kind == 'in_f':
            c0 = (0, 1024, 1536)[g]
            dma('sp', w3, win_s[:, c0:c0 + 512].rearrange("(k p) c -> p k c", p=128), KEY_WIN, [key, key2], st)
        elif kind == 'in_t':
            c0 = (512, 2048, 2560)[g]
            dma('sp', w3, win_s[:, c0:c0 + 512].rearrange("(k p) c -> p k c", p=128), KEY_WIN, [key, key2], st)
        elif kind == 'out':
            dma('sp', w3, wout_s[:, g * 512:(g + 1) * 512].rearrange("(k p) c -> p k c", p=128), KEY_WOUT, [key, key2], st)
        else:
            dma('sp', w3[:, :, 0:256], wup_s[:, g * 256:(g + 1) * 256].rearrange("(k p) c -> p k c", p=128), KEY_WUP, [key], st)
            dma('sp', w3[:, :, 256:512], wup_s[:, DFF + g * 256:DFF + (g + 1) * 256].rearrange("(k p) c -> p k c", p=128),
                KEY_WUP, [key2], ('wsu', sl))

    def need_item(n):
        while issued[0] < min(len(items), n + NWS):
            issue_item(issued[0])
            issued[0] += 1
        return slot_of[n]

    ditems = []
    for t in range(NT):
        for r_ in range(2):
            for jj in range(11):
                ditems.append(jj)
    dissued = [0]
    dslot_of = {}

    def need_ditem(n):
        while dissued[0] < min(len(ditems), n + NDS):
            m = dissued[0]
            jj = ditems[m]
            sl = m % NDS
            dslot_of[m] = sl
            dma('sp', v3(dslot[sl][:], 2), wdn_s[jj * 256:(jj + 1) * 256, :].rearrange("(j p) n -> p j n", p=128), KEY_WDN,
                [('ds', sl)], ('ds', sl))
            dissued[0] += 1
        return dslot_of[n]

    def load_x(t, chunks=range(4)):
        b, tt_ = divmod(t, TPS)
        for c in chunks:
            sl = (t % 2) * 4 + c
            r0 = b * S + tt_ * G + c * 128
            dma('sp', xs[sl][:], x_d[r0:r0 + 128, :], [], [('xs', sl)], ('xs', sl))

    def norm_stages(t, c, gm, shv, gmkey, dstA=False, evac='act'):
        b = t // TPS
        sl = (t % 2) * 4 + c
        x = xs[sl]
        s_ = sm[c % 2]
        xn_ = xn[c % 2]
        xk = ('xn', c % 2)
        smk = ('sm', c % 2)
        tb = c % 2

        def stA():
            act(xn_[:], x[:], AF.Square, [('xs', sl)], [xk, smk], accum=s_[:, 0:1])
            ts('pool', s_[:, 1:2], s_[:, 0:1], 1.0 / D, EPS, ALU.mult, ALU.add, [], [smk])
            pw(s_[:, 2:3], s_[:, 1:2], [], [smk])
            act(xn_[:], x[:], AF.Identity, [('xs', sl), smk], [xk], scale=s_[:, 2:3])

        def stB():
            for k in range(8):
                tr(Tb[tb][:, k * 128:(k + 1) * 128], xn_[:, k * 128:(k + 1) * 128], [xk], [TK(tb)])

        def stC():
            gm3 = v3(gm[:], 8)
            for k in range(8):
                o = v3((ymT if dstA else hT)[:], 8)[:, k, c * 128:(c + 1) * 128]
                hk = ('ymT', c) if dstA else ('hT', c)
                i_ = Tb[tb][:, k * 128:(k + 1) * 128]
                sc_ap = gm3[:, k, b:b + 1]
                bi_ap = modT4[:, shv, k, b:b + 1]
                if evac == 'act':
                    act(o, i_, AF.Identity, [gmkey, ('modT', shv)], [TK(tb), hk], bias=bi_ap, scale=sc_ap, nws=True)
                else:
                    P.add('dve', lambda e, o=o, i_=i_, sc_ap=sc_ap, bi_ap=bi_ap: e.tensor_scalar(out=o, in0=i_, scalar1=sc_ap, scalar2=bi_ap, op0=ALU.mult, op1=ALU.add),
                          [gmkey, ('modT', shv)], [TK(tb), hk], nws=True)
        return stA, stB, stC

    def norm_to_hT(t, c, gm, shv, gmkey, dstA=False):
        for f_ in norm_stages(t, c, gm, shv, gmkey, dstA):
            f_()

    def final_prelude(t):
        s_ = sm[1]
        for c in range(4):
            xsl = (t % 2) * 4 + c
            xn_ = xn[c % 2]; xk = ('xn', c % 2)
            act(xn_[:], xs[xsl][:], AF.Square, [('xs', xsl)], [xk, ('smf', c)], accum=s_[:, 48 + 3 * c:49 + 3 * c], nws=(c > 0))
        for c in range(4):
            o = 48 + 3 * c
            ts('pool', s_[:, o + 1:o + 2], s_[:, o:o + 1], 1.0 / D, EPS, ALU.mult, ALU.add, [], [('smf', c)])
            pw(s_[:, o + 2:o + 3], s_[:, o + 1:o + 2], [], [('smf', c)])

    def final_chunk(t, c, reload=None):
        b, tt_ = divmod(t, TPS)
        s_ = sm[1]
        xsl = (t % 2) * 4 + c
        x = xs[xsl]
        o = 48 + 3 * c
        stt(x[:], x[:], s_[:, o + 2:o + 3], FSbc[:], ALU.mult, ALU.mult, [('smf', c), 'FSbc'], [('xs', xsl)])
        tt('pool', x[:, 0:512], x[:, 0:512], FHbc[:, 0:512], ALU.add, ['FHbc', ('xs', xsl)], [('xsf', xsl, 0)])
        tt('dve', x[:, 512:1024], x[:, 512:1024], FHbc[:, 512:1024], ALU.add, ['FHbc', ('xs', xsl)], [('xsf', xsl, 1)])
        r0 = b * S + tt_ * G + c * 128
        dma('sp', out_d[r0:r0 + 128, :], x[:], [('xs', xsl), ('xsf', xsl, 0), ('xsf', xsl, 1)], [('xs', xsl)], ('out', c))
        if reload is not None:
            load_x(reload, [c])

    def final_phase(t):
        final_prelude(t)
        for c in range(4):
            final_chunk(t, c)

    HTK = [('hT', c) for c in range(4)]
    bankctr = [0]

    def nextbank(lo=0, n=2):
        bankctr[0] += 1
        return lo + (bankctr[0] % n)

    itemctr = [0]
    ditemctr = [0]

    def seq_setup_bc(b, part):
        lst = ((G1bc, G1V, 'G1bc'), (G2bc, G2V, 'G2bc')) if part == 'g' else ((FHbc, FSHV, 'FHbc'), (FSbc, FSCV, 'FSbc'))
        for (tile_, v, nm) in lst:
            dma('sp', tile_[:], mod_s[b, v * D:(v + 1) * D].partition_broadcast(128), [('mod_s', v * 4 + i_) for i_ in range(4)], [nm], ('bc', nm))
        if part == 'g':
            return
        for hh in range(2):
            dma('sp', tA[hh][:], final_g[hh * 512:(hh + 1) * 512].partition_broadcast(128), [], [('tA', hh)], ('fg', hh))
            stt(FSbc[:, hh * 512:(hh + 1) * 512], FSbc[:, hh * 512:(hh + 1) * 512], 1.0, tA[hh][:], ALU.add, ALU.mult, [('tA', hh)], ['FSbc'])

    def seq_setup_states():
        memset('dve', CT[:], 0.0, ['CT'])
        memset('dve', nst[:], 0.0, ['nst'])
        memset('dve', CTb[:], 0.0, ['CTb'])
        memset('dve', nbv[:, 0:4], 0.0, [('nbv', 0)])
        memset('dve', mprev[0][:], 0.0, [('mprev', 0)])
        memset('dve', qkhalo[:], 0.0, ['qkhalo'])
        memset('dve', ghalo[:], 0.0, ['ghalo'])

    hA3 = v3(ymT[:], 8)
    HAK = [('ymT', c) for c in range(4)]

    def A1_steps():
        steps = []
        pend_back = [None]
        cur = {}

        def blk_step(g, blk):
            def f():
                if blk == 0:
                    n = itemctr[0]; itemctr[0] += 1
                    cur['sl'] = need_item(n)
                sl = cur['sl']
                w3 = v3(wslot[sl][:], 8)
                wk = ('ws', sl); wk2 = ('ws2', sl)
                bankctr[0] += 1
                bank = (4, 5)[bankctr[0] % 2]
                for k in range(8):
                    mm(Fb[bank][:], w3[:, k, blk * 128:(blk + 1) * 128], hA3[:, k, :], k == 0, k == 7, [wk, wk2] + HAK, [FK(bank)])
                if g == 0:
                    act(v3(uT[:], 4)[:, blk, :], Fb[bank][:], AF.Gelu, [], [FK(bank), ('uT', blk)])
                else:
                    qi = (g - 1) * 4 + blk
                    cs = cst[qi % 2]
                    ck = ('cst', qi % 2)
                    ca = cacc[qi % 2]
                    cak = ('cacc', qi % 2)
                    cp('act', cs[:, 3:3 + G], Fb[bank][:], [], [FK(bank), ck])
                    cp('act', cs[:, 0:3], qkhalo[:, qi * 3:qi * 3 + 3], ['qkhalo'], [ck])
                    cp('act', qkhalo[:, qi * 3:qi * 3 + 3], cs[:, G:G + 3], [ck], ['qkhalo'])
                    cw3 = v3(cw[:], 4)
                    ts('pool', ca[:], cs[:, 0:G], cw3[:, 0, qi:qi + 1], cb[:, qi:qi + 1], ALU.mult, ALU.add, [ck, 'cw', 'cb'], [cak])
                    for tap in range(1, 4):
                        stt(ca[:], cs[:, tap:tap + G], cw3[:, tap, qi:qi + 1], ca[:], ALU.mult, ALU.add, [ck, 'cw'], [cak])
                    if pend_back[0] is not None:
                        pend_back[0]()
                    dst = qT if g == 1 else kT

                    def back(dst=dst, blk=blk, ca=ca, cak=cak, g=g):
                        act(v3(dst[:], 4)[:, blk, :], ca[:], AF.Silu, [cak], [('qk', g, blk)])
                    pend_back[0] = back
            return f

        for g in range(3):
            for blk in range(4):
                steps.append(blk_step(g, blk))

        def gates():
            wg3 = v3(wgb[:], 8)
            for k in range(8):
                mm(Fb[4][0:4, :], wg3[:, k, 0:4], hA3[:, k, :], k == 0, k == 7, ['wgb'] + HAK, [FK(4)])
            for k in range(8):
                mm(Fb[5][0:4, :], wg3[:, k, 4:8], hA3[:, k, :], k == 0, k == 7, ['wgb'] + HAK, [FK(5)])
            if pend_back[0] is not None:
                pend_back[0]()
                pend_back[0] = None
            act(irow[:], Fb[4][0:4, :], AF.Identity, ['ibt'], [FK(4), 'irow'], bias=ibt[:, 0:1])
            act(e1row[:], Fb[5][0:4, :], AF.Exp, ['nfbt'], [FK(5), 'rows'], bias=nfbt[:, 0:1], scale=-1.0)
            act(lfrow[:], e1row[:], AF.Ln, ['rows'], ['lfrow'], bias=1.0)
        steps.append(gates)
        return steps

    chunkctr = [0]

    def tile(t):
        b, tt_ = divmod(t, TPS)
        hT3 = v3(hT[:], 8)
        if t == 0:
            seq_setup_states()
            seq_setup_bc(0, 'g')
            seq_setup_bc(0, 'f')
            for c in range(4):
                norm_to_hT(t, c, gm1, SH1, 'gm1', True)
            for st_ in A1_steps():
                st_()
            if t + 1 < NT:
                load_x(t + 1)
        mp_side = []

        def mp_rows(c, cc):
            cs_ = slice(c * 128, (c + 1) * 128)
            mp = mprev[cc % 2]; mpk = ('mprev', cc % 2)
            mn = mprev[(cc + 1) % 2]; mnk = ('mprev', (cc + 1) % 2)
            RK = 'rows'
            rk = R6K[c % 2]; bk_ = BDK[c % 2]
            P.add('dve', lambda e, cs_=cs_: e.tensor_tensor_scan(out=nbrow[:, cs_], data0=ones4[:, 0:128], data1=lfrow[:, cs_], initial=0.0,
                                                                 op0=ALU.mult, op1=ALU.add), ['lfrow', 'ones4'], [RK])
            tt('dve', arow[:, cs_], irow[:, cs_], nbrow[:, cs_], ALU.add, ['irow'], [RK])
            P.add('dve', lambda e, cs_=cs_: e.tensor_tensor_scan(out=cmrow[:, cs_], data0=ones4[:, 0:128], data1=arow[:, cs_], initial=-1e30,
                                                                 op0=ALU.mult, op1=ALU.max), ['ones4'], [RK])
            ts('dve', Mrow[:, cs_], cmrow[:, cs_], mp[:, 0:1], None, ALU.max, None, [mpk], [RK])
            ML = Mrow[:, c * 128 + 127:c * 128 + 128]
            r6 = v3(r6s[c % 2], 6)
            tt('dve', mn[:, 0:1], ML, nbrow[:, c * 128 + 127:c * 128 + 128], ALU.subtract, [RK], [mnk])
            ts('dve', r6[:, 0, :], Mrow[:, cs_], -1.0, None, ALU.mult, None, [RK], rk)
            cp('dve', r6[:, 1, :], arow[:, cs_], [RK], rk)
            ts('dve', r6[:, 2, :], r6[:, 0, :], mp[:, 0:1], None, ALU.add, None, [mpk], rk)
            stt(r6[:, 3, :], r6[:, 0, :], LN_SQRT_DH, nbrow[:, cs_], ALU.add, ALU.add, [RK], rk)
            ts('dve', r6[:, 4, :], arow[:, cs_], ML, None, ALU.subtract, None, [RK], rk)
            ts('dve', r6[:, 5, :], zrow[:], mp[:, 0:1], ML, ALU.add, ALU.subtract, ['zrow', mpk, RK], rk)
            tt('dve', v3(bds[c % 2], 4), bc(r6[:, 0:1, :], [4, 4, 128]), bc(mask4[:, :, None], [4, 4, 128]), ALU.mult, ['mask4'] + rk, bk_)

        def mp_pe(c):
            r6 = v3(r6s[c % 2], 6)
            rk = R6K[c % 2]; bk_ = BDK[c % 2]
            bk = 4 + (c % 2)
            mm(Fb[bk][:], ones4[:], bds[c % 2], True, False, ['ones4'] + bk_, [FK(bk)])
            mm(Fb[bk][:], identb[:], maskb[:], False, True, ['identb', 'maskb'], [FK(bk)])
            for q in range(5):
                mm(Fb[3][:, 16 + 4 * q:20 + 4 * q], r6[:, 1 + q, :], mask4[:], True, True, rk + ['mask4'], [FK(3)])

        def mp_act(c):
            bk = 4 + (c % 2)
            cp('dve', aS4[:, c * 4:c * 4 + 4], Fb[3][:, 16:20], [], [FK(3), 'aS4'])
            act(sce4[:, c * 16:(c + 1) * 16], Fb[3][:, 20:36], AF.Exp, [], [FK(3), 'sce4'])
            for h in range(4):
                act(ptv[:, c * 512 + h * 128:c * 512 + (h + 1) * 128], Fb[bk][:, h * 128:(h + 1) * 128], AF.Exp, ['aS4'],
                    [FK(bk), ('mj', 2 * c), ('mj', 2 * c + 1)], bias=aS4[:, c * 4 + h:c * 4 + h + 1])

        ccs = []
        for c in range(4):
            ccs.append(chunkctr[0]); chunkctr[0] += 1
        R_ = lambda c: (lambda: mp_rows(c, ccs[c]))
        P_ = lambda c: (lambda: mp_pe(c))
        A_ = lambda c: (lambda: mp_act(c))
        mp_side = [R_(0), R_(1), P_(0), R_(2), A_(0), P_(1), R_(3), A_(1), P_(2), A_(2), P_(3), A_(3)]
        for g in range(3):
            n = itemctr[0]; itemctr[0] += 1
            sl = need_item(n)
            w3 = v3(wslot[sl][:], 8)
            wk = ('ws', sl); wk2 = ('ws2', sl)
            for c in range(4):
                bankctr[0] += 1
                bank = (0, 1, 2)[bankctr[0] % 3]
                for k in range(8):
                    mm(Fb[bank][:], hA3[:, k, c * 128:(c + 1) * 128], w3[:, k, :], k == 0, k == 7, [wk, wk2, ('ymT', c)], [FK(bank)])
                if mp_side:
                    mp_side.pop(0)()
                ta = tA[c % 2]; tak = ('tA', c % 2)
                tb_ = tB[c % 2]; tbk = ('tB', c % 2)
                s_ = sm[c % 2]; smk = ('sm', c % 2)
                if g == 0:
                    act(ta[:], Fb[bank][:], AF.Gelu, [], [FK(bank), tak])
                    for h in range(4):
                        act(tb_[:, h * 128:(h + 1) * 128], ta[:, h * 128:(h + 1) * 128], AF.Square, [tak], [tbk, ('ssv', c % 2, h)],
                            accum=s_[:, 8 + h:9 + h], nws=(h > 0))
                    ts('dve', s_[:, 12:16], s_[:, 8:12], 1.0 / 128, EPS, ALU.mult, ALU.add, [('ssv', c % 2, h) for h in range(4)], [smk])
                    pw(s_[:, 16:20], s_[:, 12:16], [], [smk])
                    for h in range(4):
                        hs = slice(h * 128, (h + 1) * 128)
                        P.add('dve', lambda e, ta=ta, s_=s_, h=h, hs=hs, c=c: e.scalar_tensor_tensor(out=vg[:, c * 512 + h * 128:c * 512 + (h + 1) * 128],
                              in0=ta[:, hs], scalar=s_[:, 16 + h:17 + h], in1=vnbc[:, hs], op0=ALU.mult, op1=ALU.mult),
                              [tak, smk, 'vnbc'], [('vg', c)], nws=(h > 0))
                elif g == 1:
                    cp('act', vm[:, c * 512:(c + 1) * 512], Fb[bank][:], [], [FK(bank), ('vm', c)])
                else:
                    act(ta[:], Fb[bank][:], AF.Sigmoid, [], [FK(bank), tak])
                    tt('pool', og[:, c * 512:(c + 1) * 512], ta[:], hnbc[:], ALU.mult, [tak, 'hnbc'], [('og', c)])
        while mp_side:
            mp_side.pop(0)()
        if t > 0:
            final_prelude(t - 1)
        def P1(c):
            cs_ = slice(c * 128, (c + 1) * 128)
            mm(Fb[2][:], onesrow[:], bs_hi, True, False, ['onesrow', 'bsbc'], [FK(2)])
            mm(Fb[2][:], onesrow[:], bs_lo, False, False, ['onesrow', 'bsbc'], [FK(2)])
            for h in range(4):
                mm(Fb[2][:, h * 128:(h + 1) * 128], vg[:, c * 512 + h * 128:c * 512 + (h + 1) * 128], WsT[:, h * 128:(h + 1) * 128],
                   False, h == 3, [('vg', c), 'WsT'], [FK(2)])
            ta = tA[0]; tak = ('tA', 0)
            cp('act', ta[:], Fb[2][:], [], [FK(2), tak])
            tt('pool', v3(ymT[:], 8)[:, 0:4, cs_], v3(ta[:], 4), v3(uT[:], 4)[:, :, cs_], ALU.mult,
               [tak] + [('uT', i) for i in range(4)], [('ymT', c)])
            for h in range(4):
                mm(Fb[3][:, h * 128:(h + 1) * 128], v3(kT[:], 4)[:, h, cs_], v3(qT[:], 4)[:, h, cs_], True, True,
                   [('qk', 2, h), ('qk', 1, h)], [FK(3)])
            st_ = (STb[:], STb2)[c % 2]
            tt('dve', st_, Fb[3][:], ptv[:, c * 512:(c + 1) * 512], ALU.mult, [('mj', 2 * c), ('mj', 2 * c + 1)], [FK(3), ('STb', 0) if c % 2 == 0 else ('mj', 9)])

        T1f = Tb[1][:].bitcast(F32)
        qcb = [(Fb[4][:], FK(4)), (Fb[1][:], FK(1))]
        numb = [(Fb[5][:], FK(5)), (T1f, TK(1))]

        def P2(c):
            cs_ = slice(c * 128, (c + 1) * 128)
            st_ = (STb[:], STb2)[c % 2]; stk = ('STb', 0) if c % 2 == 0 else ('mj', 9)
            qb, qbk = qcb[c % 2]
            nbk_, nbkk = numb[c % 2]
            o0 = c * 8
            for h in range(4):
                hs = slice(h * 128, (h + 1) * 128)
                mm(qb[:, hs], v3(qT[:], 4)[:, h, cs_], CTbv[c][:, hs], True, True, [('qk', 1, h), CTBK[c]], [qbk])
                mm(Fb[0][:, o0 + 4 + h:o0 + 5 + h], v3(qT[:], 4)[:, h, cs_], nbv[:, c * 4 + h:c * 4 + h + 1], True, True,
                   [('qk', 1, h), ('nbv', c)], [FK(0)])
            for h in range(4):
                hs = slice(h * 128, (h + 1) * 128)
                mm(nbk_[:, hs], st_[:, hs], vm[:, c * 512 + h * 128:c * 512 + (h + 1) * 128], True, True, [stk, ('vm', c)], [nbkk])
                mm(Fb[0][:, o0 + h:o0 + h + 1], st_[:, hs], onescol[:], True, True, [stk, 'onescol'], [FK(0)])

        def S_front(c):
            cs_ = slice(c * 128, (c + 1) * 128)
            wv = sce4[:, c * 16 + 8:c * 16 + 12]
            tt('pool', v3(vw[:], 4), v3(vm[:, c * 512:(c + 1) * 512], 4), bc(wv[:, :, None], [128, 4, 128]), ALU.mult,
               [('vm', c), 'sce4'], ['vw'])
            cp('pool', wcol[:], wv, ['sce4'], ['wcol'])
            for h in range(4):
                tr(Tb[0][:, h * 128:(h + 1) * 128], v3(kT[:], 4)[:, h, cs_], [('qk', 2, h)], [TK(0)])
            cp('act', ktok[:], Tb[0][:, 0:512], [], [TK(0), 'ktok'])
            for h in range(4):
                hs = slice(h * 128, (h + 1) * 128)
                mm(Fb[1][:, hs], ktok[:, hs], vw[:, hs], True, True, ['ktok', 'vw'], [FK(1)])
                mm(Fb[0][:, 40 + h:41 + h], ktok[:, hs], wcol[:, h:h + 1], True, True, ['ktok', 'wcol'], [FK(0)])

        def S_dve(c):
            dec = sce4[:, c * 16 + 12:c * 16 + 16]
            for h in range(4):
                hs = slice(h * 128, (h + 1) * 128)
                P.add('dve', lambda e, h=h, hs=hs, dec=dec: e.scalar_tensor_tensor(out=CT[:, hs], in0=CT[:, hs], scalar=dec[:, h:h + 1], in1=Fb[1][:, hs],
                      op0=ALU.mult, op1=ALU.add), ['sce4'], [FK(1), 'CT'], nws=(h > 0))
            tt('dve', nst[:], nst[:], dec, ALU.mult, ['sce4'], ['nst'])
            tt('dve', nst[:], Fb[0][:, 40:44], nst[:], ALU.add, [], [FK(0), 'nst'])

        def S_back(c):
            nv = (c + 1) % 4
            cp('act', CTbv[nv], CT[:], ['CT'], [CTBK[nv]])
            cp('act', nbv[:, nv * 4:nv * 4 + 4], nst[:], ['nst'], [('nbv', nv)])

        def P4(c, part):
            iw = sce4[:, c * 16:c * 16 + 4]; en = sce4[:, c * 16 + 4:c * 16 + 8]
            p_ = c % 2
            t1 = t1s[p_]; t1k = T1K[p_]
            t2 = t2s[p_]; t2k = T2K[p_]
            qb, qbk = qcb[p_]
            nbk_, nbkk = numb[p_]
            o0 = c * 8
            s_ = sm[p_]; smk = ('sm', p_)
            sq = ss2[:, p_ * 4:p_ * 4 + 4]
            if part >= 1:
                P4b(c, p_, t1, t1k, t2, t2k, s_, smk, sq, part)
                return
            tt('dve', v3(t1, 4), v3(qb, 4), bc(iw[:, :, None], [128, 4, 128]), ALU.mult, ['sce4'], [qbk] + t1k)
            tt('dve', t2, nbk_, t1, ALU.add, t1k, [nbkk] + t2k)
            tt('dve', s_[:, 20:24], Fb[0][:, o0 + 4:o0 + 8], iw, ALU.mult, ['sce4'], [FK(0), smk])
            tt('dve', s_[:, 20:24], Fb[0][:, o0:o0 + 4], s_[:, 20:24], ALU.add, [], [FK(0), smk])
            ts('dve', s_[:, 24:28], s_[:, 20:24], -1.0, None, ALU.mult, None, [], [smk])
            tt('dve', s_[:, 24:28], s_[:, 24:28], s_[:, 20:24], ALU.max, [], [smk])
            tt('dve', s_[:, 24:28], s_[:, 24:28], en, ALU.max, ['sce4'], [smk])
            tt('dve', s_[:, 28:32], s_[:, 24:28], s_[:, 24:28], ALU.mult, [], [smk])
            for h in range(4):
                act(t1[:, h * 128:(h + 1) * 128], t2[:, h * 128:(h + 1) * 128], AF.Square, t2k, t1k + [('ss2', p_, h)],
                    accum=sq[:, h:h + 1], nws=True)

        def P4b(c, p_, t1, t1k, t2, t2k, s_, smk, sq, part):
            if part == 2:
                P4c(c, t2, t2k, s_, smk)
                return
            stt(s_[:, 36:40], s_[:, 28:32], 128.0 * EPS, sq, ALU.mult, ALU.add, [('ss2', p_, h) for h in range(4)], [smk])
            pw(s_[:, 44:48], s_[:, 36:40], [], [smk])

        def P4c(c, t2, t2k, s_, smk):
            yb = (yml[:], yml2)[c % 2]; ybk = ('yml', c % 2) if c % 2 == 0 else ('mj', 8)
            for h in range(4):
                hs = slice(h * 128, (h + 1) * 128)
                P.add('dve', lambda e, yb=yb, t2=t2, s_=s_, h=h, hs=hs, c=c: e.scalar_tensor_tensor(out=yb[:, hs], in0=t2[:, hs], scalar=s_[:, 44 + h:45 + h],
                      in1=og[:, c * 512 + h * 128:c * 512 + (h + 1) * 128], op0=ALU.mult, op1=ALU.mult), t2k + [smk, ('og', c)], [ybk], nws=(h > 0))

        def P5(c):
            cs_ = slice(c * 128, (c + 1) * 128)
            yb = (yml[:], yml2)[c % 2]; ybk = ('yml', c % 2) if c % 2 == 0 else ('mj', 8)
            for h in range(4):
                tr(Tb[0][:, h * 128:(h + 1) * 128], yb[:, h * 128:(h + 1) * 128], [ybk], [TK(0)])
            cp('act', v3(ymT[:], 8)[:, 4:8, cs_], v3(Tb[0][:, 0:512], 4), [], [TK(0), ('ymT', c)])

        P1(0); P2(0); P1(1); P4(0, 0)
        S_front(0); S_dve(0)
        S_front(1); S_back(0); S_dve(1)
        S_front(2); S_back(1); S_dve(2); S_back(2)
        P2(1)
        rl = (t + 1) if t + 1 < NT else None
        fin = (lambda c: final_chunk(t - 1, c, rl)) if t > 0 else (lambda c: None)
        P1(2); P4(0, 1); P4(1, 0); P4(0, 2); P2(2); P5(0)
        P1(3); P4(1, 1); P4(2, 0); P4(1, 2); P2(3); P5(1)
        P4(2, 1); P4(3, 0); P4(2, 2); P5(2)
        P4(3, 1); P4(3, 2); P5(3)
        S_front(3); S_dve(3); S_back(3)
        if t > 0 and tt_ == 0:
            seq_setup_bc(b, 'g')
        YK = [('ymT', c) for c in range(4)]
        n0 = itemctr[0]; itemctr[0] += 2
        sls = [need_item(n0), None]
        sls[1] = slot_of[n0 + 1]
        pend = []
        for c in range(4):
            xsl = (t % 2) * 4 + c
            for half in range(2):
                sl = sls[half]
                w3 = v3(wslot[sl][:], 8)
                wk = ('ws', sl); wk2 = ('ws2', sl)
                bank = nextbank()
                for k in range(8):
                    mm(Fb[bank][:], v3(ymT[:], 8)[:, k, c * 128:(c + 1) * 128], w3[:, k, :], k == 0, k == 7, [wk, wk2, ('ymT', c)], [FK(bank)])
                ta = tA[half]; tak = ('tA', half)
                hsl = slice(half * 512, (half + 1) * 512)
                tt('dve', ta[:], Fb[bank][:], G1bc[:, hsl], ALU.mult, ['G1bc'], [FK(bank), tak])
                tt('pool', xs[xsl][:, hsl], xs[xsl][:, hsl], ta[:], ALU.add, [tak], [('xs', xsl)])
            fin(c)
            for f_ in pend:
                f_()
            stA, stB, stC = norm_stages(t, c, gm2, SH2, 'gm2', evac='dve')
            stA()
            pend = [stB, stC]
        for f_ in pend:
            f_()
        if t > 0 and tt_ == 0:
            seq_setup_bc(b, 'f')
        fw3 = v3(fw[:], 3)
        b1_back = [None]
        side = []
        if t + 1 < NT:
            st = [norm_stages(t + 1, c, gm1, SH1, 'gm1', True) for c in range(4)]
            side = [st[0][0], st[1][0], st[0][1], st[2][0], st[0][2], st[1][1], st[3][0], st[1][2], st[2][1], st[2][2], st[3][1], st[3][2]]
        for jj in range(11):
            n = itemctr[0]; itemctr[0] += 1
            sl = need_item(n)
            w3 = v3(wslot[sl][:], 8)
            wk = ('ws', sl); wk2 = ('ws2', sl)
            for jb in range(2):
                j = jj * 2 + jb
                gb = (0, 1, 4)[j % 3]
                ub = (2, 3, 5)[j % 3]
                for k in range(8):
                    mm(Fb[gb][:], w3[:, k, jb * 128:(jb + 1) * 128], hT3[:, k, :], k == 0, k == 7, [wk] + HTK, [FK(gb)])
                for k in range(8):
                    mm(Fb[ub][:], w3[:, k, 256 + jb * 128:256 + (jb + 1) * 128], hT3[:, k, :], k == 0, k == 7, [wk2] + HTK, [FK(ub)])
                gs = gst[j % 2]; gk = ('gst', j % 2)
                ca = cacc[j % 2]; cak = ('cacc', j % 2)
                cp('act', gs[:, 2:2 + G], Fb[gb][:], [], [FK(gb), gk])
                cp('pool', gs[:, 0:2], ghalo[:, j * 2:j * 2 + 2], ['ghalo'], [gk])
                cp('pool', ghalo[:, j * 2:j * 2 + 2], gs[:, G:G + 2], [gk], ['ghalo'])
                ts('dve', ca[:], gs[:, 0:G], fw3[:, 0, j:j + 1], fb[:, j:j + 1], ALU.mult, ALU.add, [gk, 'fw', 'fb'], [cak])
                for tap in range(1, 3):
                    stt(ca[:], gs[:, tap:tap + G], fw3[:, tap, j:j + 1], ca[:], ALU.mult, ALU.add, [gk, 'fw'], [cak])
                if b1_back[0] is not None:
                    b1_back[0]()

                def back(j=j, ca=ca, cak=cak, ub=ub):
                    ta = tA[j % 2]; tak = ('tA', j % 2)
                    act(ta[:], ca[:], AF.Gelu, [cak], [tak])
                    tt('dve', v3(mj[:], NJ)[:, j, :], Fb[ub][:], ta[:], ALU.mult, [tak], [FK(ub), ('mj', j)])
                b1_back[0] = back
                if j >= 2 and side:
                    side.pop(0)()
        if b1_back[0] is not None:
            b1_back[0]()
            b1_back[0] = None
        while side:
            side.pop(0)()
        a1s = []
        if t + 1 < NT:
            if (t + 1) % TPS == 0:
                seq_setup_states()
            a1s = A1_steps()
        for r in range(2):
            for jj in range(11):
                n = ditemctr[0]; ditemctr[0] += 1
                sl = need_ditem(n)
                d3 = v3(dslot[sl][:], 2)
                dk = ('ds', sl)
                for jb in range(2):
                    j = jj * 2 + jb
                    for ci in range(2):
                        c = r * 2 + ci
                        for half in range(2):
                            bank = ci * 2 + half
                            mm(Fb[bank][:], v3(mj[:], NJ)[:, j, c * 128:(c + 1) * 128], d3[:, jb, half * 512:(half + 1) * 512],
                               j == 0, j == NJ - 1, [dk, ('mj', j)], [FK(bank)])
                if a1s:
                    a1s.pop(0)()
            for ci in range(2):
                c = r * 2 + ci
                xsl = (t % 2) * 4 + c
                for half in range(2):
                    bank = ci * 2 + half
                    ta = tA[half]; tak = ('tA', half)
                    hsl = slice(half * 512, (half + 1) * 512)
                    tt('dve', ta[:], Fb[bank][:], G2bc[:, hsl], ALU.mult, ['G2bc'], [FK(bank), tak])
                    tt('pool', xs[xsl][:, hsl], xs[xsl][:, hsl], ta[:], ALU.add, [tak], [('xs', xsl)])
        while a1s:
            a1s.pop(0)()

    load_x(0)
    for t in range(NT):
        tile(t)
    final_phase(NT - 1)

    with nc.allow_non_contiguous_dma(reason="small parameter loads"):
        nw = P.emit(nc, es)
    es.close()
    return nc, len(P.ops), nw


_CACHE = {}


def kernel(**inputs):
    NSEQ = 4
    if 'nc' not in _CACHE:
        _CACHE['nc'] = build(NSEQ, 4)[0]
    nc = _CACHE['nc']
    f = lambda a: np.ascontiguousarray(np.asarray(a, dtype=np.float32))
    x = f(inputs['x']); c = f(inputs['c'])
    shared = {
        'ada_w': f(inputs['ada_w']).reshape(D, 6 * D),
        'ada_b': f(inputs['ada_b']).reshape(6 * D),
        'norm1_g': f(inputs['norm1_g']).reshape(D),
        'w_in': f(inputs['w_in']).reshape(D, NIN),
        'gm_vnorm_g': f(inputs['gm_vnorm_g']).reshape(512),
        'gm_spatial_w': f(inputs['gm_spatial_w']).reshape(4, 128, 128),
        'gm_spatial_b': f(inputs['gm_spatial_b']).reshape(512),
        'ml_conv_w': f(inputs['ml_conv_w']).reshape(4, 1024),
        'ml_conv_b': f(inputs['ml_conv_b']).reshape(1024),
        'ml_i_b': f(inputs['ml_i_b']).reshape(4, 1),
        'ml_f_b': f(inputs['ml_f_b']).reshape(4, 1),
        'ml_hnorm_g': f(inputs['ml_hnorm_g']).reshape(512),
        'w_out': f(inputs['w_out']).reshape(D, D),
        'norm2_g': f(inputs['norm2_g']).reshape(D),
        'ffn_w_up': f(inputs['ffn_w_up']).reshape(D, 2 * DFF),
        'ffn_conv_w': f(inputs['ffn_conv_w']).reshape(3, DFF),
        'ffn_conv_b': f(inputs['ffn_conv_b']).reshape(DFF),
        'ffn_w_down': f(inputs['ffn_w_down']).reshape(DFF, D),
        'final_ada_w': f(inputs['final_ada_w']).reshape(D, 2 * D),
        'final_ada_b': f(inputs['final_ada_b']).reshape(2 * D),
        'final_g': f(inputs['final_g']).reshape(D),
    }
    in_maps = []
    for i in range(NCORE):
        m = dict(shared)
        m['x'] = x[i * NSEQ:(i + 1) * NSEQ].reshape(NSEQ * S, D)
        m['c'] = c[i * NSEQ:(i + 1) * NSEQ]
        in_maps.append(m)
    res = run_bass_kernel_spmd(nc, in_maps, core_ids=list(range(NCORE)))
    out = np.concatenate([r['out'].reshape(NSEQ, S, D) for r in res.results], axis=0)
    return out.astype(np.float32)
```

```python
import math
from contextlib import ExitStack

import numpy as np
import concourse.bass as bass
import concourse.mybir as mybir
from concourse.bass_utils import run_bass_kernel_spmd

F32 = mybir.dt.float32
BF16 = mybir.dt.bfloat16
AF = mybir.ActivationFunctionType
ALU = mybir.AluOpType
AX = mybir.AxisListType

D = 1024
S = 2048
NCORE = 8
DFF = 2816
NJ = 22
NIN = 3080
G = 512
EPS = 1e-6
LN_SQRT_DH = 0.5 * math.log(128.0)


class Prog:
    def __init__(self):
        self.ops = []
        self.last_w = {}
        self.readers = {}

    def add(self, eng, fn, reads=(), writes=(), dma=None, nws=False):
        i = len(self.ops)
        deps = {}
        wset = set(writes)
        for k in reads:
            if k in wset:
                continue
            w = self.last_w.get(k)
            if w is not None:
                deps[w] = True
        for k in wset:
            w = self.last_w.get(k)
            if w is not None:
                deps[w] = True
            for r in self.readers.get(k, ()):
                if r not in deps:
                    deps[r] = False
        for k in wset:
            self.last_w[k] = i
            self.readers[k] = []
        for k in reads:
            if k in wset:
                continue
            lst = self.readers.setdefault(k, [])
            if dma is None:
                lst[:] = [r for r in lst if not (self.ops[r]['dma'] is None and self.ops[r]['eng'] == eng)]
            lst.append(i)
        self.ops.append(dict(eng=eng, fn=fn, deps=deps, dma=dma, nws=nws))
        return i

    def emit(self, nc, es):
        engs = {'pe': nc.tensor, 'act': nc.scalar, 'dve': nc.vector, 'pool': nc.gpsimd, 'sp': nc.sync}
        ops = self.ops
        def needs_wait(op, d, hard):
            dop = ops[d]
            if dop['dma'] is not None or op['dma'] is not None and dop['eng'] != op['eng']:
                return True
            if dop['eng'] != op['eng']:
                return True
            if op['eng'] == 'pe' or op['nws']:
                return False
            return hard
        signal = set()
        for op in ops:
            for d, hard in op['deps'].items():
                if needs_wait(op, d, hard):
                    signal.add(d)
        sem_e = {e: es.enter_context(nc.semaphore("se_" + e)) for e in engs}
        dma_streams = sorted({op['dma'] for op in ops if op['dma'] is not None}, key=str)
        sem_d = {s: es.enter_context(nc.semaphore("sd_%d" % n)) for n, s in enumerate(dma_streams)}
        cnt_e = {e: 0 for e in engs}
        cnt_d = {s: 0 for s in dma_streams}
        token = {}
        seen = {e: {} for e in engs}
        nwait = 0
        for i, op in enumerate(ops):
            eng = op['eng']
            e = engs[eng]
            need = {}
            for d, hard in op['deps'].items():
                if not needs_wait(op, d, hard):
                    continue
                sem, val, skey = token[d]
                if need.get(skey, (None, 0))[1] < val:
                    need[skey] = (sem, val)
            for skey, (sem, val) in need.items():
                if seen[eng].get(skey, 0) >= val:
                    continue
                e.wait_ge(sem, val)
                seen[eng][skey] = val
                nwait += 1
            ins = op['fn'](e)
            if op['dma'] is not None:
                s = op['dma']
                cnt_d[s] += 16
                ins.then_inc(sem_d[s], 16)
                token[i] = (sem_d[s], cnt_d[s], ('d', s))
            elif i in signal:
                cnt_e[eng] += 1
                ins.then_inc(sem_e[eng], 1)
                token[i] = (sem_e[eng], cnt_e[eng], ('e', eng))
                seen[eng][('e', eng)] = max(seen[eng].get(('e', eng), 0), 0)
        for s in dma_streams:
            if cnt_d[s] > 0 and str(s).startswith("('out'"):
                nc.sync.wait_ge(sem_d[s], cnt_d[s])
        return nwait


def build(NSEQ=4, TPS=4, debug=False):
    nc = bass.Bass("TRN2", target_bir_lowering=False)
    NT = NSEQ * TPS
    NTOK = NSEQ * S

    def din(name, shape, dt=F32):
        return nc.dram_tensor(name, shape, dt, kind="ExternalInput").ap()

    x_d = din("x", [NTOK, D])
    c_d = din("c", [NSEQ, D])
    ada_w = din("ada_w", [D, 6 * D])
    ada_b = din("ada_b", [6 * D])
    norm1_g = din("norm1_g", [D])
    w_in = din("w_in", [D, NIN])
    vnorm_g = din("gm_vnorm_g", [512])
    sp_w = din("gm_spatial_w", [4, 128, 128])
    sp_b = din("gm_spatial_b", [512])
    mlcw = din("ml_conv_w", [4, 1024])
    mlcb = din("ml_conv_b", [1024])
    ml_ib = din("ml_i_b", [4, 1])
    ml_fb = din("ml_f_b", [4, 1])
    hnorm_g = din("ml_hnorm_g", [512])
    w_out = din("w_out", [D, D])
    norm2_g = din("norm2_g", [D])
    w_up = din("ffn_w_up", [D, 2 * DFF])
    fcw = din("ffn_conv_w", [3, DFF])
    fcb = din("ffn_conv_b", [DFF])
    w_dn = din("ffn_w_down", [DFF, D])
    fada_w = din("final_ada_w", [D, 2 * D])
    fada_b = din("final_ada_b", [2 * D])
    final_g = din("final_g", [D])
    out_d = nc.dram_tensor("out", [NTOK, D], F32, kind="ExternalOutput").ap()

    def dscr(name, shape, dt):
        return nc.dram_tensor(name, shape, dt, kind="Internal").ap()

    win_s = dscr("win_s", [D, NIN], BF16)
    wout_s = dscr("wout_s", [D, D], BF16)
    wup_s = dscr("wup_s", [D, 2 * DFF], BF16)
    wdn_s = dscr("wdn_s", [DFF, D], BF16)
    mod_s = dscr("mod_s", [NSEQ, 8 * D], F32)

    P = Prog()
    es = ExitStack()

    def sb(name, shape, dt=F32):
        return es.enter_context(nc.sbuf_tensor(name, shape, dt))

    def ps(name, shape, dt=F32):
        return es.enter_context(nc.psum_tensor(name, shape, dt))

    NWS = 3
    wslot = [sb("wslot%d" % i, [128, 4096], BF16) for i in range(NWS)]
    NDS = 3
    dslot = [sb("dslot%d" % i, [128, 2048], BF16) for i in range(NDS)]
    xs = [sb("xs%d" % i, [128, D]) for i in range(8)]
    hT = sb("hT", [128, 8 * G], BF16)
    uT = sb("uT", [128, 4 * G], BF16)
    qT = sb("qT", [128, 4 * G], BF16)
    kT = sb("kT", [128, 4 * G], BF16)
    cst = [sb("cst%d" % i, [128, 3 + G]) for i in range(2)]
    cacc = [sb("cacc%d" % i, [128, G]) for i in range(2)]
    qkhalo = sb("qkhalo", [128, 8 * 3])
    vg = sb("vg", [128, 4 * 512], BF16)
    vm = sb("vm", [128, 4 * 512], BF16)
    og = sb("og", [128, 4 * 512], BF16)
    ymT = sb("ymT", [128, 8 * G], BF16)
    mj = sb("mj", [128, NJ * G], BF16)
    gst = [sb("gst%d" % i, [128, 2 + G]) for i in range(2)]
    ghalo = sb("ghalo", [128, NJ * 2])
    tA = [sb("tA%d" % i, [128, 512]) for i in range(2)]
    tB = [sb("tB%d" % i, [128, 512]) for i in range(2)]
    xn = [sb("xn%d" % i, [128, D], BF16) for i in range(2)]
    sm = [sb("sm%d" % i, [128, 64]) for i in range(2)]
    G1bc = sb("G1bc", [128, D]); G2bc = sb("G2bc", [128, D]); FSbc = sb("FSbc", [128, D]); FHbc = sb("FHbc", [128, D])
    vnbc = sb("vnbc", [128, 512]); hnbc = sb("hnbc", [128, 512]); bsbc = sb("bsbc", [128, 512])
    modT = sb("modT", [128, 8 * 8 * NSEQ])
    n1g = sb("n1g", [128, 8]); n2g = sb("n2g", [128, 8])
    gm1 = sb("gm1", [128, 8 * NSEQ]); gm2 = sb("gm2", [128, 8 * NSEQ])
    cw = sb("cw", [128, 4 * 8]); cb = sb("cb", [128, 8])
    fw = sb("fw", [128, 3 * NJ]); fb = sb("fb", [128, NJ])
    wg32 = sb("wg32", [128, 64]); wgb = sb("wgb", [128, 64], BF16)
    cT = sb("cT", [128, 8 * NSEQ])
    modrow = [sb("modrow%d" % i, [NSEQ, 256]) for i in range(2)]; biasrow = [sb("biasrow%d" % i, [NSEQ, 256]) for i in range(2)]
    ident32 = sb("ident32", [128, 128]); identb = sb("identb", [128, 128], BF16)
    maskb = sb("maskb", [128, 512], BF16)
    ones4 = sb("ones4", [4, 128]); mask4 = sb("mask4", [4, 4]); zrow = sb("zrow", [4, 128])
    onescol = sb("onescol", [128, 1], BF16); negh = sb("negh", [128, 4]); onesrow = sb("onesrow", [1, 128], BF16)
    WsT = sb("WsT", [128, 512], BF16)
    ibt = sb("ibt", [4, 1]); fbt = sb("fbt", [4, 1]); nfbt = sb("nfbt", [4, 1])
    irow = sb("irow", [4, G]); lfrow = sb("lfrow", [4, G])
    nbrow = sb("nbrow", [4, G]); arow = sb("arow", [4, G]); cmrow = sb("cmrow", [4, G]); e1row = arow; Mrow = cmrow
    mprev = [sb("mprev%d" % i, [4, 1]) for i in range(2)]
    CT = sb("CT", [128, 512]); nst = sb("nst", [128, 4]); CTb = sb("CTb", [128, 512], BF16)
    pt = sb("pt", [128, 512]); STb = sb("STb", [128, 512], BF16)
    vw = sb("vw", [128, 512], BF16); wcol = sb("wcol", [128, 4], BF16); ktok = sb("ktok", [128, 512], BF16)
    yml = sb("yml", [128, 512], BF16)
    aS4 = sb("aS4", [128, 16]); sce4 = sb("sce4", [128, 64]); ss2 = sb("ss2", [128, 8]); nbv = sb("nbv", [128, 16], BF16)

    mb32 = tA[0]; wsp32 = tA[1]; wspb = yml
    ptv = mj[:, 0:4096].bitcast(F32)
    yml2 = mj[:, 4096:4608]
    STb2 = mj[:, 4608:5120]
    r6s = [mj[0:4, 7680:9216].bitcast(F32), mj[0:4, 5120:6656].bitcast(F32)]
    t1s = [tA[1][:], mj[:, 7680:8704].bitcast(F32)]
    t2s = [tB[1][:], mj[:, 8704:9728].bitcast(F32)]
    T1K = [[('tA', 1)], [('mj', 15), ('mj', 16)]]
    T2K = [[('tB', 1)], [('mj', 17), ('mj', 18)]]
    CTbv = [CTb[:], mj[:, 9728:10240], mj[:, 10240:10752], mj[:, 10752:11264]]
    CTBK = ['CTb', ('mj', 19), ('mj', 20), ('mj', 21)]
    bds = [mj[0:4, 9216:10240].bitcast(F32), mj[0:4, 6656:7680].bitcast(F32)]
    R6K = [[('mj', 15), ('mj', 16), ('mj', 17)], [('mj', 10), ('mj', 11), ('mj', 12)]]
    BDK = [[('mj', 18), ('mj', 19)], [('mj', 13), ('mj', 14)]]
    Fb = [ps("F%d" % i, [128, 512]) for i in range(6)]
    Tb = [ps("T%d" % i, [128, 1024], BF16) for i in range(2)]

    def FK(i):
        return ('F', i)

    def TK(i):
        return ('T', i)

    def v3(ap, a):
        return ap.rearrange("p (a b) -> p a b", a=a)

    def bc(ap, shape):
        return ap.to_broadcast(shape)

    def act(out, in_, func, reads, writes, bias=None, scale=None, accum=None, nws=False):
        kw = {}
        if bias is not None:
            kw['bias'] = bias
        if scale is not None:
            kw['scale'] = scale
        if accum is not None:
            kw['accum_out'] = accum
        P.add('act', lambda e: e.activation(out=out, in_=in_, func=func, **kw), reads, writes, nws=nws)

    def tt(eng, out, in0, in1, op, reads, writes):
        P.add(eng, lambda e: e.tensor_tensor(out=out, in0=in0, in1=in1, op=op), reads, writes)

    def ts(eng, out, in0, s1, s2, op0, op1, reads, writes):
        if s2 is None:
            P.add(eng, lambda e: e.tensor_scalar(out=out, in0=in0, scalar1=s1, scalar2=None, op0=op0), reads, writes)
        else:
            P.add(eng, lambda e: e.tensor_scalar(out=out, in0=in0, scalar1=s1, scalar2=s2, op0=op0, op1=op1), reads, writes)

    def stt(out, in0, scalar, in1, op0, op1, reads, writes):
        P.add('dve', lambda e: e.scalar_tensor_tensor(out=out, in0=in0, scalar=scalar, in1=in1, op0=op0, op1=op1), reads, writes)

    def cp(eng, out, in_, reads, writes):
        if eng == 'act':
            P.add('act', lambda e: e.copy(out=out, in_=in_), reads, writes)
        else:
            P.add(eng, lambda e: e.tensor_copy(out=out, in_=in_), reads, writes)

    def mm(out, lhsT, rhs, start, stop, reads, writes):
        P.add('pe', lambda e: e.matmul(out, lhsT=lhsT, rhs=rhs, start=start, stop=stop, skip_group_check=True), reads, writes)

    def tr(out, in_, reads, writes):
        P.add('pe', lambda e: e.transpose(out, in_, identb[:]), reads + ['identb'], writes)

    def dma(q, out, in_, reads, writes, stream):
        P.add(q, lambda e: e.dma_start(out=out, in_=in_), reads, writes, dma=stream)

    def memset(eng, ap, val, writes):
        P.add(eng, lambda e: e.memset(ap, val), [], writes)

    def pw(out, in_, reads, writes):
        n = out.shape[-1]
        P.add('pool', lambda e: e.tensor_tensor(out=out, in0=in_, in1=negh[0:out.shape[0], 0:n], op=ALU.pow), reads + ['negh'], writes)

    KEY_WIN = [('win_s', i) for i in range(4)]
    KEY_WOUT = [('wout_s', i) for i in range(4)]
    KEY_WUP = [('wup_s', i) for i in range(4)]
    KEY_WDN = [('wdn_s', i) for i in range(4)]

    memset('dve', ident32[:], 0.0, ['ident32'])
    P.add('pool', lambda e: e.affine_select(out=ident32[:], in_=ident32[:], pattern=[[-1, 128]], compare_op=ALU.not_equal,
                                            fill=1.0, base=0, channel_multiplier=1), [], ['ident32'])
    cp('dve', identb[:], ident32[:], ['ident32'], ['identb'])
    memset('dve', mb32[:], 0.0, [('tA', 0)])
    P.add('pool', lambda e: e.affine_select(out=v3(mb32[:], 4), in_=v3(mb32[:], 4), pattern=[[0, 4], [1, 128]], compare_op=ALU.is_ge,
                                            fill=-30000.0, base=0, channel_multiplier=-1), [], [('tA', 0)])
    cp('dve', maskb[:], mb32[:], [('tA', 0)], ['maskb'])
    memset('dve', ones4[:], 1.0, ['ones4'])
    memset('dve', zrow[:], 0.0, ['zrow'])
    memset('dve', mask4[:], 0.0, ['mask4'])
    P.add('pool', lambda e: e.affine_select(out=mask4[:], in_=mask4[:], pattern=[[-1, 4]], compare_op=ALU.not_equal,
                                            fill=1.0, base=0, channel_multiplier=1), [], ['mask4'])
    memset('dve', onescol[:], 1.0, ['onescol'])
    memset('dve', negh[:], -0.5, ['negh'])
    dma('sp', v3(wsp32[:], 4), sp_w.rearrange("h t s -> t h s"), [], [('tA', 1)], ('misc', 0))
    P.add('pool', lambda e: e.affine_select(out=v3(wsp32[:], 4), in_=v3(wsp32[:], 4), pattern=[[0, 4], [-1, 128]], compare_op=ALU.is_ge,
                                            fill=0.0, base=0, channel_multiplier=1), [], [('tA', 1)])
    cp('dve', wspb[:], wsp32[:], [('tA', 1)], ['yml'])
    for h in range(4):
        tr(Tb[0][:, h * 128:(h + 1) * 128], wspb[:, h * 128:(h + 1) * 128], ['yml'], [TK(0)])
    cp('dve', WsT[:], Tb[0][:, 0:512], [], [TK(0), 'WsT'])
    dma('sp', vnbc[:], vnorm_g.partition_broadcast(128), [], ['vnbc'], ('misc', 1))
    dma('sp', hnbc[:], hnorm_g.partition_broadcast(128), [], ['hnbc'], ('misc', 2))
    ts('dve', hnbc[:], hnbc[:], math.sqrt(128.0), None, ALU.mult, None, [], ['hnbc'])
    bsh = bsbc[:].bitcast(BF16)
    bs_hi = bsh[0:1, 0:512]; bs_lo = bsh[0:1, 512:1024]
    dma('sp', tA[0][0:1, :], sp_b.rearrange("(o n) -> o n", o=1), [], [('tA', 0)], ('misc', 3))
    cp('dve', bs_hi, tA[0][0:1, :], [('tA', 0)], ['bsbc'])
    tt('dve', tA[1][0:1, :], tA[0][0:1, :], bs_hi, ALU.subtract, [('tA', 0), 'bsbc'], [('tA', 1)])
    cp('dve', bs_lo, tA[1][0:1, :], [('tA', 1)], ['bsbc'])
    memset('dve', onesrow[:], 1.0, ['onesrow'])
    dma('sp', n1g[:], norm1_g.rearrange("(k p) -> p k", p=128), [], ['n1g'], ('misc', 5))
    dma('sp', n2g[:], norm2_g.rearrange("(k p) -> p k", p=128), [], ['n2g'], ('misc', 6))
    dma('sp', v3(cw[:], 4), mlcw.rearrange("t (k p) -> p t k", p=128), [], ['cw'], ('misc', 7))
    dma('sp', cb[:], mlcb.rearrange("(k p) -> p k", p=128), [], ['cb'], ('misc', 8))
    dma('sp', v3(fw[:], 3), fcw.rearrange("t (k p) -> p t k", p=128), [], ['fw'], ('misc', 9))
    dma('sp', fb[:], fcb.rearrange("(k p) -> p k", p=128), [], ['fb'], ('misc', 10))
    dma('sp', ibt[:], ml_ib, [], ['ibt'], ('misc', 11))
    dma('sp', fbt[:], ml_fb, [], ['fbt'], ('misc', 12))
    ts('dve', nfbt[:], fbt[:], -1.0, None, ALU.mult, None, ['fbt'], ['nfbt'])
    dma('sp', v3(wg32[:], 8), w_in[:, 3072:3080].rearrange("(k p) c -> p k c", p=128), [], ['wg32'], ('misc', 13))
    cp('dve', wgb[:], wg32[:], ['wg32'], ['wgb'])
    for b_ in range(NSEQ):
        dma('sp', v3(cT[:], 8)[:, :, b_], c_d[b_, :].rearrange("(k p) -> p k", p=128), [], [('cTl', b_)], ('cT', b_))
    act(cT[:], cT[:], AF.Silu, [('cTl', b_) for b_ in range(NSEQ)], ['cT'])
    ngrp = 32
    for g in range(ngrp):
        sl = g % NWS
        w32 = wslot[sl][:].bitcast(F32).rearrange("p (k c) -> p k c", k=8)
        if g < 24:
            src = ada_w[:, g * 256:(g + 1) * 256]
        else:
            src = fada_w[:, (g - 24) * 256:(g - 23) * 256]
        dma('sp', w32, src.rearrange("(k p) c -> p k c", p=128), [], [('ws', sl), ('ws2', sl)] + (['adaload'] if g == ngrp - 1 else []), ('ws', sl))
        bank = 4 + (g % 2)
        for k in range(8):
            mm(Fb[bank][0:NSEQ, 0:256], v3(cT[:], 8)[:, k, :], w32[:, k, :], k == 0, k == 7, ['cT', ('ws', sl), ('ws2', sl)], [FK(bank)])
        bsrc = ada_b[g * 256:(g + 1) * 256] if g < 24 else fada_b[(g - 24) * 256:(g - 23) * 256]
        dma('act', biasrow[g % 2][:], bsrc.partition_broadcast(NSEQ), [], [('biasrow', g % 2)], ('brow', g % 2))
        tt('dve', modrow[g % 2][:], Fb[bank][0:NSEQ, 0:256], biasrow[g % 2][:], ALU.add,
           [('biasrow', g % 2)], [FK(bank), ('modrow', g % 2)])
        dma('act', mod_s[:, g * 256:(g + 1) * 256], modrow[g % 2][:], [('modrow', g % 2)], [('mod_s', g)], ('mrow', g % 2))
        v_ = g // 4
        if v_ in (0, 1, 3, 4):
            for kk in range(2):
                k_ = (g % 4) * 2 + kk
                col = (v_ * 8 + k_) * NSEQ
                mm(Fb[3][:, col:col + NSEQ], modrow[g % 2][:, kk * 128:(kk + 1) * 128], mask4[0:NSEQ, 0:NSEQ], True, True,
                   [('modrow', g % 2), 'mask4'], [FK(3)])
    modT4 = modT[:].rearrange("p (v k b) -> p v k b", v=8, k=8)
    cp('dve', modT[:, 0:2 * 8 * NSEQ], Fb[3][:, 0:2 * 8 * NSEQ], [], [FK(3)] + [('modT', v) for v in range(8)])
    cp('dve', modT[:, 3 * 8 * NSEQ:5 * 8 * NSEQ], Fb[3][:, 3 * 8 * NSEQ:5 * 8 * NSEQ], [], [FK(3)] + [('modT', v) for v in range(8)])
    for (dst, src, nm, rows, rd) in ((win_s, w_in, 'win_s', D, ['adaload']), (wout_s, w_out, 'wout_s', D, ['adaload']),
                                     (wup_s, w_up, 'wup_s', D, KEY_WIN + KEY_WOUT), (wdn_s, w_dn, 'wdn_s', DFF, KEY_WIN + KEY_WOUT)):
        nsp = 4 if rows % 4 == 0 else 2
        rr = rows // nsp
        for i in range(nsp):
            dma('pool', dst[i * rr:(i + 1) * rr, :], src[i * rr:(i + 1) * rr, :], rd, [(nm, i)], ('cast', nm, i))
    for (gm, ng, vi, nm, ngn) in ((gm1, n1g, 1, 'gm1', 'n1g'), (gm2, n2g, 4, 'gm2', 'n2g')):
        ts('dve', v3(gm[:], 8), modT4[:, vi, :, :], 1.0, None, ALU.add, None, [('modT', vi)], [nm])
        tt('dve', v3(gm[:], 8), v3(gm[:], 8), bc(ng[:, :, None], [128, 8, NSEQ]), ALU.mult, [ngn], [nm])
    SH1, SH2 = 0, 3
    G1V, G2V, FSHV, FSCV = 2, 5, 6, 7

    items = []
    for t in range(NT):
        for g in range(3):
            items.append(('in_f', g))
        for g in range(3):
            items.append(('in_t', g))
        items.append(('out', 0))
        items.append(('out', 1))
        for jj in range(11):
            items.append(('up', jj))
    issued = [0]
    slot_of = {}

    def issue_item(n):
        kind, g = items[n]
        sl = n % NWS
        slot_of[n] = sl
        w3 = v3(wslot[sl][:], 8)
        key = ('ws', sl)
        key2 = ('ws2', sl)
        st = ('ws', sl)
        if kind == 'in_f':
            c0 = (0, 1024, 1536)[g]
            dma('sp', w3, win_s[:, c0:c0 + 512].rearrange("(k p) c -> p k c", p=128), KEY_WIN, [key, key2], st)
        elif kind == 'in_t':
            c0 = (512, 2048, 2560)[g]
            dma('sp', w3, win_s[:, c0:c0 + 512].rearrange("(k p) c -> p k c", p=128), KEY_WIN, [key, key2], st)
        elif kind == 'out':
            dma('sp', w3, wout_s[:, g * 512:(g + 1) * 512].rearrange("(k p) c -> p k c", p=128), KEY_WOUT, [key, key2], st)
        else:
            dma('sp', w3[:, :, 0:256], wup_s[:, g * 256:(g + 1) * 256].rearrange("(k p) c -> p k c", p=128), KEY_WUP, [key], st)
            dma('sp', w3[:, :, 256:512], wup_s[:, DFF + g * 256:DFF + (g + 1) * 256].rearrange("(k p) c -> p k c", p=128),
                KEY_WUP, [key2], ('wsu', sl))

    def need_item(n):
        while issued[0] < min(len(items), n + NWS):
            issue_item(issued[0])
            issued[0] += 1
        return slot_of[n]

    ditems = []
    for t in range(NT):
        for r_ in range(2):
            for jj in range(11):
                ditems.append(jj)
    dissued = [0]
    dslot_of = {}

    def need_ditem(n):
        while dissued[0] < min(len(ditems), n + NDS):
            m = dissued[0]
            jj = ditems[m]
            sl = m % NDS
            dslot_of[m] = sl
            dma('sp', v3(dslot[sl][:], 2), wdn_s[jj * 256:(jj + 1) * 256, :].rearrange("(j p) n -> p j n", p=128), KEY_WDN,
                [('ds', sl)], ('ds', sl))
            dissued[0] += 1
        return dslot_of[n]

    def load_x(t, chunks=range(4)):
        b, tt_ = divmod(t, TPS)
        for c in chunks:
            sl = (t % 2) * 4 + c
            r0 = b * S + tt_ * G + c * 128
            dma('sp', xs[sl][:], x_d[r0:r0 + 128, :], [], [('xs', sl)], ('xs', sl))

    def norm_stages(t, c, gm, shv, gmkey, dstA=False, evac='act'):
        b = t // TPS
        sl = (t % 2) * 4 + c
        x = xs[sl]
        s_ = sm[c % 2]
        xn_ = xn[c % 2]
        xk = ('xn', c % 2)
        smk = ('sm', c % 2)
        tb = c % 2

        def stA():
            act(xn_[:], x[:], AF.Square, [('xs', sl)], [xk, smk], accum=s_[:, 0:1])
            ts('pool', s_[:, 1:2], s_[:, 0:1], 1.0 / D, EPS, ALU.mult, ALU.add, [], [smk])
            pw(s_[:, 2:3], s_[:, 1:2], [], [smk])
            act(xn_[:], x[:], AF.Identity, [('xs', sl), smk], [xk], scale=s_[:, 2:3])

        def stB():
            for k in range(8):
                tr(Tb[tb][:, k * 128:(k + 1) * 128], xn_[:, k * 128:(k + 1) * 128], [xk], [TK(tb)])

        def stC():
            gm3 = v3(gm[:], 8)
            for k in range(8):
                o = v3((ymT if dstA else hT)[:], 8)[:, k, c * 128:(c + 1) * 128]
                hk = ('ymT', c) if dstA else ('hT', c)
                i_ = Tb[tb][:, k * 128:(k + 1) * 128]
                sc_ap = gm3[:, k, b:b + 1]
                bi_ap = modT4[:, shv, k, b:b + 1]
                if evac == 'act':
                    act(o, i_, AF.Identity, [gmkey, ('modT', shv)], [TK(tb), hk], bias=bi_ap, scale=sc_ap, nws=True)
                else:
                    P.add('dve', lambda e, o=o, i_=i_, sc_ap=sc_ap, bi_ap=bi_ap: e.tensor_scalar(out=o, in0=i_, scalar1=sc_ap, scalar2=bi_ap, op0=ALU.mult, op1=ALU.add),
                          [gmkey, ('modT', shv)], [TK(tb), hk], nws=True)
        return stA, stB, stC

    def norm_to_hT(t, c, gm, shv, gmkey, dstA=False):
        for f_ in norm_stages(t, c, gm, shv, gmkey, dstA):
            f_()

    def final_prelude(t):
        s_ = sm[1]
        for c in range(4):
            xsl = (t % 2) * 4 + c
            xn_ = xn[c % 2]; xk = ('xn', c % 2)
            act(xn_[:], xs[xsl][:], AF.Square, [('xs', xsl)], [xk, ('smf', c)], accum=s_[:, 48 + 3 * c:49 + 3 * c])
        for c in range(4):
            o = 48 + 3 * c
            ts('pool', s_[:, o + 1:o + 2], s_[:, o:o + 1], 1.0 / D, EPS, ALU.mult, ALU.add, [], [('smf', c)])
            pw(s_[:, o + 2:o + 3], s_[:, o + 1:o + 2], [], [('smf', c)])

    def final_chunk(t, c, reload=None):
        b, tt_ = divmod(t, TPS)
        s_ = sm[1]
        xsl = (t % 2) * 4 + c
        x = xs[xsl]
        o = 48 + 3 * c
        stt(x[:], x[:], s_[:, o + 2:o + 3], FSbc[:], ALU.mult, ALU.mult, [('smf', c), 'FSbc'], [('xs', xsl)])
        tt('pool', x[:, 0:512], x[:, 0:512], FHbc[:, 0:512], ALU.add, ['FHbc', ('xs', xsl)], [('xsf', xsl, 0)])
        tt('dve', x[:, 512:1024], x[:, 512:1024], FHbc[:, 512:1024], ALU.add, ['FHbc', ('xs', xsl)], [('xsf', xsl, 1)])
        r0 = b * S + tt_ * G + c * 128
        dma('sp', out_d[r0:r0 + 128, :], x[:], [('xs', xsl), ('xsf', xsl, 0), ('xsf', xsl, 1)], [('xs', xsl)], ('out', c))
        if reload is not None:
            load_x(reload, [c])

    def final_phase(t):
        final_prelude(t)
        for c in range(4):
            final_chunk(t, c)

    HTK = [('hT', c) for c in range(4)]
    bankctr = [0]

    def nextbank(lo=0, n=2):
        bankctr[0] += 1
        return lo + (bankctr[0] % n)

    itemctr = [0]
    ditemctr = [0]

    def seq_setup_bc(b, part):
        lst = ((G1bc, G1V, 'G1bc'), (G2bc, G2V, 'G2bc')) if part == 'g' else ((FHbc, FSHV, 'FHbc'), (FSbc, FSCV, 'FSbc'))
        for (tile_, v, nm) in lst:
            dma('sp', tile_[:], mod_s[b, v * D:(v + 1) * D].partition_broadcast(128), [('mod_s', v * 4 + i_) for i_ in range(4)], [nm], ('bc', nm))
        if part == 'g':
            return
        for hh in range(2):
            dma('sp', tA[hh][:], final_g[hh * 512:(hh + 1) * 512].partition_broadcast(128), [], [('tA', hh)], ('fg', hh))
            stt(FSbc[:, hh * 512:(hh + 1) * 512], FSbc[:, hh * 512:(hh + 1) * 512], 1.0, tA[hh][:], ALU.add, ALU.mult, [('tA', hh)], ['FSbc'])

    def seq_setup_states():
        memset('dve', CT[:], 0.0, ['CT'])
        memset('dve', nst[:], 0.0, ['nst'])
        memset('dve', CTb[:], 0.0, ['CTb'])
        memset('dve', nbv[:, 0:4], 0.0, [('nbv', 0)])
        memset('dve', mprev[0][:], 0.0, [('mprev', 0)])
        memset('dve', qkhalo[:], 0.0, ['qkhalo'])
        memset('dve', ghalo[:], 0.0, ['ghalo'])

    hA3 = v3(ymT[:], 8)
    HAK = [('ymT', c) for c in range(4)]

    def A1_steps():
        steps = []
        pend_back = [None]
        cur = {}

        def blk_step(g, blk):
            def f():
                if blk == 0:
                    n = itemctr[0]; itemctr[0] += 1
                    cur['sl'] = need_item(n)
                sl = cur['sl']
                w3 = v3(wslot[sl][:], 8)
                wk = ('ws', sl); wk2 = ('ws2', sl)
                bankctr[0] += 1
                bank = (4, 5)[bankctr[0] % 2]
                for k in range(8):
                    mm(Fb[bank][:], w3[:, k, blk * 128:(blk + 1) * 128], hA3[:, k, :], k == 0, k == 7, [wk, wk2] + HAK, [FK(bank)])
                if g == 0:
                    act(v3(uT[:], 4)[:, blk, :], Fb[bank][:], AF.Gelu, [], [FK(bank), ('uT', blk)])
                else:
                    qi = (g - 1) * 4 + blk
                    cs = cst[qi % 2]
                    ck = ('cst', qi % 2)
                    ca = cacc[qi % 2]
                    cak = ('cacc', qi % 2)
                    cp('act', cs[:, 3:3 + G], Fb[bank][:], [], [FK(bank), ck])
                    cp('act', cs[:, 0:3], qkhalo[:, qi * 3:qi * 3 + 3], ['qkhalo'], [ck])
                    cp('act', qkhalo[:, qi * 3:qi * 3 + 3], cs[:, G:G + 3], [ck], ['qkhalo'])
                    cw3 = v3(cw[:], 4)
                    ts('pool', ca[:], cs[:, 0:G], cw3[:, 0, qi:qi + 1], cb[:, qi:qi + 1], ALU.mult, ALU.add, [ck, 'cw', 'cb'], [cak])
                    for tap in range(1, 4):
                        stt(ca[:], cs[:, tap:tap + G], cw3[:, tap, qi:qi + 1], ca[:], ALU.mult, ALU.add, [ck, 'cw'], [cak])
                    if pend_back[0] is not None:
                        pend_back[0]()
                    dst = qT if g == 1 else kT

                    def back(dst=dst, blk=blk, ca=ca, cak=cak, g=g):
                        act(v3(dst[:], 4)[:, blk, :], ca[:], AF.Silu, [cak], [('qk', g, blk)])
                    pend_back[0] = back
            return f

        for g in range(3):
            for blk in range(4):
                steps.append(blk_step(g, blk))

        def gates():
            wg3 = v3(wgb[:], 8)
            for k in range(8):
                mm(Fb[4][0:4, :], wg3[:, k, 0:4], hA3[:, k, :], k == 0, k == 7, ['wgb'] + HAK, [FK(4)])
            for k in range(8):
                mm(Fb[5][0:4, :], wg3[:, k, 4:8], hA3[:, k, :], k == 0, k == 7, ['wgb'] + HAK, [FK(5)])
            if pend_back[0] is not None:
                pend_back[0]()
                pend_back[0] = None
            act(irow[:], Fb[4][0:4, :], AF.Identity, ['ibt'], [FK(4), 'irow'], bias=ibt[:, 0:1])
            act(e1row[:], Fb[5][0:4, :], AF.Exp, ['nfbt'], [FK(5), 'rows'], bias=nfbt[:, 0:1], scale=-1.0)
            act(lfrow[:], e1row[:], AF.Ln, ['rows'], ['lfrow'], bias=1.0)
        steps.append(gates)
        return steps

    chunkctr = [0]

    def tile(t):
        b, tt_ = divmod(t, TPS)
        hT3 = v3(hT[:], 8)
        if t == 0:
            seq_setup_states()
            seq_setup_bc(0, 'g')
            seq_setup_bc(0, 'f')
            for c in range(4):
                norm_to_hT(t, c, gm1, SH1, 'gm1', True)
            for st_ in A1_steps():
                st_()
            if t + 1 < NT:
                load_x(t + 1)
        mp_side = []

        def mp_rows(c, cc):
            cs_ = slice(c * 128, (c + 1) * 128)
            mp = mprev[cc % 2]; mpk = ('mprev', cc % 2)
            mn = mprev[(cc + 1) % 2]; mnk = ('mprev', (cc + 1) % 2)
            RK = 'rows'
            rk = R6K[c % 2]; bk_ = BDK[c % 2]
            P.add('dve', lambda e, cs_=cs_: e.tensor_tensor_scan(out=nbrow[:, cs_], data0=ones4[:, 0:128], data1=lfrow[:, cs_], initial=0.0,
                                                                 op0=ALU.mult, op1=ALU.add), ['lfrow', 'ones4'], [RK])
            tt('dve', arow[:, cs_], irow[:, cs_], nbrow[:, cs_], ALU.add, ['irow'], [RK])
            P.add('dve', lambda e, cs_=cs_: e.tensor_tensor_scan(out=cmrow[:, cs_], data0=ones4[:, 0:128], data1=arow[:, cs_], initial=-1e30,
                                                                 op0=ALU.mult, op1=ALU.max), ['ones4'], [RK])
            ts('dve', Mrow[:, cs_], cmrow[:, cs_], mp[:, 0:1], None, ALU.max, None, [mpk], [RK])
            ML = Mrow[:, c * 128 + 127:c * 128 + 128]
            r6 = v3(r6s[c % 2], 6)
            tt('dve', mn[:, 0:1], ML, nbrow[:, c * 128 + 127:c * 128 + 128], ALU.subtract, [RK], [mnk])
            ts('dve', r6[:, 0, :], Mrow[:, cs_], -1.0, None, ALU.mult, None, [RK], rk)
            cp('dve', r6[:, 1, :], arow[:, cs_], [RK], rk)
            ts('dve', r6[:, 2, :], r6[:, 0, :], mp[:, 0:1], None, ALU.add, None, [mpk], rk)
            stt(r6[:, 3, :], r6[:, 0, :], LN_SQRT_DH, nbrow[:, cs_], ALU.add, ALU.add, [RK], rk)
            ts('dve', r6[:, 4, :], arow[:, cs_], ML, None, ALU.subtract, None, [RK], rk)
            ts('dve', r6[:, 5, :], zrow[:], mp[:, 0:1], ML, ALU.add, ALU.subtract, ['zrow', mpk, RK], rk)
            tt('dve', v3(bds[c % 2], 4), bc(r6[:, 0:1, :], [4, 4, 128]), bc(mask4[:, :, None], [4, 4, 128]), ALU.mult, ['mask4'] + rk, bk_)

        def mp_pe(c):
            r6 = v3(r6s[c % 2], 6)
            rk = R6K[c % 2]; bk_ = BDK[c % 2]
            bk = 4 + (c % 2)
            mm(Fb[bk][:], ones4[:], bds[c % 2], True, False, ['ones4'] + bk_, [FK(bk)])
            mm(Fb[bk][:], identb[:], maskb[:], False, True, ['identb', 'maskb'], [FK(bk)])
            for q in range(5):
                mm(Fb[3][:, 16 + 4 * q:20 + 4 * q], r6[:, 1 + q, :], mask4[:], True, True, rk + ['mask4'], [FK(3)])

        def mp_act(c):
            bk = 4 + (c % 2)
            cp('dve', aS4[:, c * 4:c * 4 + 4], Fb[3][:, 16:20], [], [FK(3), 'aS4'])
            act(sce4[:, c * 16:(c + 1) * 16], Fb[3][:, 20:36], AF.Exp, [], [FK(3), 'sce4'])
            for h in range(4):
                act(ptv[:, c * 512 + h * 128:c * 512 + (h + 1) * 128], Fb[bk][:, h * 128:(h + 1) * 128], AF.Exp, ['aS4'],
                    [FK(bk), ('mj', 2 * c), ('mj', 2 * c + 1)], bias=aS4[:, c * 4 + h:c * 4 + h + 1])

        ccs = []
        for c in range(4):
            ccs.append(chunkctr[0]); chunkctr[0] += 1
        R_ = lambda c: (lambda: mp_rows(c, ccs[c]))
        P_ = lambda c: (lambda: mp_pe(c))
        A_ = lambda c: (lambda: mp_act(c))
        mp_side = [R_(0), R_(1), P_(0), R_(2), A_(0), P_(1), R_(3), A_(1), P_(2), A_(2), P_(3), A_(3)]
        for g in range(3):
            n = itemctr[0]; itemctr[0] += 1
            sl = need_item(n)
            w3 = v3(wslot[sl][:], 8)
            wk = ('ws', sl); wk2 = ('ws2', sl)
            for c in range(4):
                bankctr[0] += 1
                bank = (0, 1, 2)[bankctr[0] % 3]
                for k in range(8):
                    mm(Fb[bank][:], hA3[:, k, c * 128:(c + 1) * 128], w3[:, k, :], k == 0, k == 7, [wk, wk2, ('ymT', c)], [FK(bank)])
                if mp_side:
                    mp_side.pop(0)()
                ta = tA[c % 2]; tak = ('tA', c % 2)
                tb_ = tB[c % 2]; tbk = ('tB', c % 2)
                s_ = sm[c % 2]; smk = ('sm', c % 2)
                if g == 0:
                    act(ta[:], Fb[bank][:], AF.Gelu, [], [FK(bank), tak])
                    for h in range(4):
                        act(tb_[:, h * 128:(h + 1) * 128], ta[:, h * 128:(h + 1) * 128], AF.Square, [tak], [tbk, ('ssv', c % 2, h)],
                            accum=s_[:, 8 + h:9 + h], nws=(h > 0))
                    ts('dve', s_[:, 12:16], s_[:, 8:12], 1.0 / 128, EPS, ALU.mult, ALU.add, [('ssv', c % 2, h) for h in range(4)], [smk])
                    pw(s_[:, 16:20], s_[:, 12:16], [], [smk])
                    for h in range(4):
                        hs = slice(h * 128, (h + 1) * 128)
                        P.add('dve', lambda e, ta=ta, s_=s_, h=h, hs=hs, c=c: e.scalar_tensor_tensor(out=vg[:, c * 512 + h * 128:c * 512 + (h + 1) * 128],
                              in0=ta[:, hs], scalar=s_[:, 16 + h:17 + h], in1=vnbc[:, hs], op0=ALU.mult, op1=ALU.mult),
                              [tak, smk, 'vnbc'], [('vg', c)], nws=(h > 0))
                elif g == 1:
                    cp('act', vm[:, c * 512:(c + 1) * 512], Fb[bank][:], [], [FK(bank), ('vm', c)])
                else:
                    act(ta[:], Fb[bank][:], AF.Sigmoid, [], [FK(bank), tak])
                    tt('pool', og[:, c * 512:(c + 1) * 512], ta[:], hnbc[:], ALU.mult, [tak, 'hnbc'], [('og', c)])
        while mp_side:
            mp_side.pop(0)()
        if t > 0:
            final_prelude(t - 1)
        def P1(c):
            cs_ = slice(c * 128, (c + 1) * 128)
            mm(Fb[2][:], onesrow[:], bs_hi, True, False, ['onesrow', 'bsbc'], [FK(2)])
            mm(Fb[2][:], onesrow[:], bs_lo, False, False, ['onesrow', 'bsbc'], [FK(2)])
            for h in range(4):
                mm(Fb[2][:, h * 128:(h + 1) * 128], vg[:, c * 512 + h * 128:c * 512 + (h + 1) * 128], WsT[:, h * 128:(h + 1) * 128],
                   False, h == 3, [('vg', c), 'WsT'], [FK(2)])
            ta = tA[0]; tak = ('tA', 0)
            cp('act', ta[:], Fb[2][:], [], [FK(2), tak])
            tt('pool', v3(ymT[:], 8)[:, 0:4, cs_], v3(ta[:], 4), v3(uT[:], 4)[:, :, cs_], ALU.mult,
               [tak] + [('uT', i) for i in range(4)], [('ymT', c)])
            for h in range(4):
                mm(Fb[3][:, h * 128:(h + 1) * 128], v3(kT[:], 4)[:, h, cs_], v3(qT[:], 4)[:, h, cs_], True, True,
                   [('qk', 2, h), ('qk', 1, h)], [FK(3)])
            st_ = (STb[:], STb2)[c % 2]
            tt('dve', st_, Fb[3][:], ptv[:, c * 512:(c + 1) * 512], ALU.mult, [('mj', 2 * c), ('mj', 2 * c + 1)], [FK(3), ('STb', 0) if c % 2 == 0 else ('mj', 9)])

        T1f = Tb[1][:].bitcast(F32)
        qcb = [(Fb[4][:], FK(4)), (Fb[1][:], FK(1))]
        numb = [(Fb[5][:], FK(5)), (T1f, TK(1))]

        def P2(c):
            cs_ = slice(c * 128, (c + 1) * 128)
            st_ = (STb[:], STb2)[c % 2]; stk = ('STb', 0) if c % 2 == 0 else ('mj', 9)
            qb, qbk = qcb[c % 2]
            nbk_, nbkk = numb[c % 2]
            o0 = c * 8
            for h in range(4):
                hs = slice(h * 128, (h + 1) * 128)
                mm(qb[:, hs], v3(qT[:], 4)[:, h, cs_], CTbv[c][:, hs], True, True, [('qk', 1, h), CTBK[c]], [qbk])
                mm(Fb[0][:, o0 + 4 + h:o0 + 5 + h], v3(qT[:], 4)[:, h, cs_], nbv[:, c * 4 + h:c * 4 + h + 1], True, True,
                   [('qk', 1, h), ('nbv', c)], [FK(0)])
            for h in range(4):
                hs = slice(h * 128, (h + 1) * 128)
                mm(nbk_[:, hs], st_[:, hs], vm[:, c * 512 + h * 128:c * 512 + (h + 1) * 128], True, True, [stk, ('vm', c)], [nbkk])
                mm(Fb[0][:, o0 + h:o0 + h + 1], st_[:, hs], onescol[:], True, True, [stk, 'onescol'], [FK(0)])

        def S_front(c):
            cs_ = slice(c * 128, (c + 1) * 128)
            wv = sce4[:, c * 16 + 8:c * 16 + 12]
            tt('pool', v3(vw[:], 4), v3(vm[:, c * 512:(c + 1) * 512], 4), bc(wv[:, :, None], [128, 4, 128]), ALU.mult,
               [('vm', c), 'sce4'], ['vw'])
            cp('pool', wcol[:], wv, ['sce4'], ['wcol'])
            for h in range(4):
                tr(Tb[0][:, h * 128:(h + 1) * 128], v3(kT[:], 4)[:, h, cs_], [('qk', 2, h)], [TK(0)])
            cp('act', ktok[:], Tb[0][:, 0:512], [], [TK(0), 'ktok'])
            for h in range(4):
                hs = slice(h * 128, (h + 1) * 128)
                mm(Fb[1][:, hs], ktok[:, hs], vw[:, hs], True, True, ['ktok', 'vw'], [FK(1)])
                mm(Fb[0][:, 40 + h:41 + h], ktok[:, hs], wcol[:, h:h + 1], True, True, ['ktok', 'wcol'], [FK(0)])

        def S_dve(c):
            dec = sce4[:, c * 16 + 12:c * 16 + 16]
            for h in range(4):
                hs = slice(h * 128, (h + 1) * 128)
                P.add('dve', lambda e, h=h, hs=hs, dec=dec: e.scalar_tensor_tensor(out=CT[:, hs], in0=CT[:, hs], scalar=dec[:, h:h + 1], in1=Fb[1][:, hs],
                      op0=ALU.mult, op1=ALU.add), ['sce4'], [FK(1), 'CT'], nws=(h > 0))
            tt('dve', nst[:], nst[:], dec, ALU.mult, ['sce4'], ['nst'])
            tt('dve', nst[:], Fb[0][:, 40:44], nst[:], ALU.add, [], [FK(0), 'nst'])

        def S_back(c):
            nv = (c + 1) % 4
            cp('act', CTbv[nv], CT[:], ['CT'], [CTBK[nv]])
            cp('act', nbv[:, nv * 4:nv * 4 + 4], nst[:], ['nst'], [('nbv', nv)])

        def P4(c, part):
            iw = sce4[:, c * 16:c * 16 + 4]; en = sce4[:, c * 16 + 4:c * 16 + 8]
            p_ = c % 2
            t1 = t1s[p_]; t1k = T1K[p_]
            t2 = t2s[p_]; t2k = T2K[p_]
            qb, qbk = qcb[p_]
            nbk_, nbkk = numb[p_]
            o0 = c * 8
            s_ = sm[p_]; smk = ('sm', p_)
            sq = ss2[:, p_ * 4:p_ * 4 + 4]
            if part >= 1:
                P4b(c, p_, t1, t1k, t2, t2k, s_, smk, sq, part)
                return
            tt('dve', v3(t1, 4), v3(qb, 4), bc(iw[:, :, None], [128, 4, 128]), ALU.mult, ['sce4'], [qbk] + t1k)
            tt('dve', t2, nbk_, t1, ALU.add, t1k, [nbkk] + t2k)
            tt('dve', s_[:, 20:24], Fb[0][:, o0 + 4:o0 + 8], iw, ALU.mult, ['sce4'], [FK(0), smk])
            tt('dve', s_[:, 20:24], Fb[0][:, o0:o0 + 4], s_[:, 20:24], ALU.add, [], [FK(0), smk])
            ts('dve', s_[:, 24:28], s_[:, 20:24], -1.0, None, ALU.mult, None, [], [smk])
            tt('dve', s_[:, 24:28], s_[:, 24:28], s_[:, 20:24], ALU.max, [], [smk])
            tt('dve', s_[:, 24:28], s_[:, 24:28], en, ALU.max, ['sce4'], [smk])
            tt('dve', s_[:, 28:32], s_[:, 24:28], s_[:, 24:28], ALU.mult, [], [smk])
            for h in range(4):
                act(t1[:, h * 128:(h + 1) * 128], t2[:, h * 128:(h + 1) * 128], AF.Square, t2k, t1k + [('ss2', p_, h)],
                    accum=sq[:, h:h + 1], nws=True)

        def P4b(c, p_, t1, t1k, t2, t2k, s_, smk, sq, part):
            if part == 2:
                P4c(c, t2, t2k, s_, smk)
                return
            stt(s_[:, 36:40], s_[:, 28:32], 128.0 * EPS, sq, ALU.mult, ALU.add, [('ss2', p_, h) for h in range(4)], [smk])
            pw(s_[:, 44:48], s_[:, 36:40], [], [smk])

        def P4c(c, t2, t2k, s_, smk):
            yb = (yml[:], yml2)[c % 2]; ybk = ('yml', c % 2) if c % 2 == 0 else ('mj', 8)
            for h in range(4):
                hs = slice(h * 128, (h + 1) * 128)
                P.add('dve', lambda e, yb=yb, t2=t2, s_=s_, h=h, hs=hs, c=c: e.scalar_tensor_tensor(out=yb[:, hs], in0=t2[:, hs], scalar=s_[:, 44 + h:45 + h],
                      in1=og[:, c * 512 + h * 128:c * 512 + (h + 1) * 128], op0=ALU.mult, op1=ALU.mult), t2k + [smk, ('og', c)], [ybk], nws=(h > 0))

        def P5(c):
            cs_ = slice(c * 128, (c + 1) * 128)
            yb = (yml[:], yml2)[c % 2]; ybk = ('yml', c % 2) if c % 2 == 0 else ('mj', 8)
            for h in range(4):
                tr(Tb[0][:, h * 128:(h + 1) * 128], yb[:, h * 128:(h + 1) * 128], [ybk], [TK(0)])
            cp('act', v3(ymT[:], 8)[:, 4:8, cs_], v3(Tb[0][:, 0:512], 4), [], [TK(0), ('ymT', c)])

        S_front(0); S_dve(0)
        S_front(1); S_back(0); S_dve(1)
        S_front(2); S_back(1); S_dve(2)
        P1(0); S_back(2); P2(0)
        P1(1); P2(1)
        rl = (t + 1) if t + 1 < NT else None
        fin = (lambda c: final_chunk(t - 1, c, rl)) if t > 0 else (lambda c: None)
        P4(0, 0); P1(2); P4(0, 1); P4(1, 0); P4(0, 2); P2(2); P5(0)
        P1(3); P4(1, 1); P4(2, 0); P4(1, 2); P2(3); P5(1)
        P4(2, 1); P4(3, 0); P4(2, 2); P5(2)
        P4(3, 1); P4(3, 2); P5(3)
        S_front(3); S_dve(3); S_back(3)
        if t > 0 and tt_ == 0:
            seq_setup_bc(b, 'g')
        YK = [('ymT', c) for c in range(4)]
        n0 = itemctr[0]; itemctr[0] += 2
        sls = [need_item(n0), None]
        sls[1] = slot_of[n0 + 1]
        pend = []
        for c in range(4):
            xsl = (t % 2) * 4 + c
            for half in range(2):
                sl = sls[half]
                w3 = v3(wslot[sl][:], 8)
                wk = ('ws', sl); wk2 = ('ws2', sl)
                bank = nextbank()
                for k in range(8):
                    mm(Fb[bank][:], v3(ymT[:], 8)[:, k, c * 128:(c + 1) * 128], w3[:, k, :], k == 0, k == 7, [wk, wk2, ('ymT', c)], [FK(bank)])
                ta = tA[half]; tak = ('tA', half)
                hsl = slice(half * 512, (half + 1) * 512)
                tt('dve', ta[:], Fb[bank][:], G1bc[:, hsl], ALU.mult, ['G1bc'], [FK(bank), tak])
                tt('pool', xs[xsl][:, hsl], xs[xsl][:, hsl], ta[:], ALU.add, [tak], [('xs', xsl)])
            fin(c)
            for f_ in pend:
                f_()
            stA, stB, stC = norm_stages(t, c, gm2, SH2, 'gm2', evac='dve')
            stA()
            pend = [stB, stC]
        for f_ in pend:
            f_()
        if t > 0 and tt_ == 0:
            seq_setup_bc(b, 'f')
        fw3 = v3(fw[:], 3)
        b1_back = [None]
        side = []
        if t + 1 < NT:
            st = [norm_stages(t + 1, c, gm1, SH1, 'gm1', True) for c in range(4)]
            side = [st[0][0], st[1][0], st[0][1], st[2][0], st[0][2], st[1][1], st[3][0], st[1][2], st[2][1], st[2][2], st[3][1], st[3][2]]
        for jj in range(11):
            n = itemctr[0]; itemctr[0] += 1
            sl = need_item(n)
            w3 = v3(wslot[sl][:], 8)
            wk = ('ws', sl); wk2 = ('ws2', sl)
            for jb in range(2):
                j = jj * 2 + jb
                gb = (0, 1, 4)[j % 3]
                ub = (2, 3, 5)[j % 3]
                for k in range(8):
                    mm(Fb[gb][:], w3[:, k, jb * 128:(jb + 1) * 128], hT3[:, k, :], k == 0, k == 7, [wk] + HTK, [FK(gb)])
                for k in range(8):
                    mm(Fb[ub][:], w3[:, k, 256 + jb * 128:256 + (jb + 1) * 128], hT3[:, k, :], k == 0, k == 7, [wk2] + HTK, [FK(ub)])
                gs = gst[j % 2]; gk = ('gst', j % 2)
                ca = cacc[j % 2]; cak = ('cacc', j % 2)
                cp('act', gs[:, 2:2 + G], Fb[gb][:], [], [FK(gb), gk])
                cp('pool', gs[:, 0:2], ghalo[:, j * 2:j * 2 + 2], ['ghalo'], [gk])
                cp('pool', ghalo[:, j * 2:j * 2 + 2], gs[:, G:G + 2], [gk], ['ghalo'])
                ts('dve', ca[:], gs[:, 0:G], fw3[:, 0, j:j + 1], fb[:, j:j + 1], ALU.mult, ALU.add, [gk, 'fw', 'fb'], [cak])
                for tap in range(1, 3):
                    stt(ca[:], gs[:, tap:tap + G], fw3[:, tap, j:j + 1], ca[:], ALU.mult, ALU.add, [gk, 'fw'], [cak])
                if b1_back[0] is not None:
                    b1_back[0]()

                def back(j=j, ca=ca, cak=cak, ub=ub):
                    ta = tA[j % 2]; tak = ('tA', j % 2)
                    act(ta[:], ca[:], AF.Gelu, [cak], [tak])
                    tt('dve', v3(mj[:], NJ)[:, j, :], Fb[ub][:], ta[:], ALU.mult, [tak], [FK(ub), ('mj', j)])
                b1_back[0] = back
                if j >= 2 and side:
                    side.pop(0)()
        if b1_back[0] is not None:
            b1_back[0]()
            b1_back[0] = None
        while side:
            side.pop(0)()
        a1s = []
        if t + 1 < NT:
            if (t + 1) % TPS == 0:
                seq_setup_states()
            a1s = A1_steps()
        for r in range(2):
            for jj in range(11):
                n = ditemctr[0]; ditemctr[0] += 1
                sl = need_ditem(n)
                d3 = v3(dslot[sl][:], 2)
                dk = ('ds', sl)
                for jb in range(2):
                    j = jj * 2 + jb
                    for ci in range(2):
                        c = r * 2 + ci
                        for half in range(2):
                            bank = ci * 2 + half
                            mm(Fb[bank][:], v3(mj[:], NJ)[:, j, c * 128:(c + 1) * 128], d3[:, jb, half * 512:(half + 1) * 512],
                               j == 0, j == NJ - 1, [dk, ('mj', j)], [FK(bank)])
                if a1s:
                    a1s.pop(0)()
            for ci in range(2):
                c = r * 2 + ci
                xsl = (t % 2) * 4 + c
                for half in range(2):
                    bank = ci * 2 + half
                    ta = tA[half]; tak = ('tA', half)
                    hsl = slice(half * 512, (half + 1) * 512)
                    tt('dve', ta[:], Fb[bank][:], G2bc[:, hsl], ALU.mult, ['G2bc'], [FK(bank), tak])
                    tt('pool', xs[xsl][:, hsl], xs[xsl][:, hsl], ta[:], ALU.add, [tak], [('xs', xsl)])
        while a1s:
            a1s.pop(0)()

    load_x(0)
    for t in range(NT):
        tile(t)
    final_phase(NT - 1)

    with nc.allow_non_contiguous_dma(reason="small parameter loads"):
        nw = P.emit(nc, es)
    es.close()
    return nc, len(P.ops), nw


_CACHE = {}


def kernel(**inputs):
    NSEQ = 4
    if 'nc' not in _CACHE:
        _CACHE['nc'] = build(NSEQ, 4)[0]
    nc = _CACHE['nc']
    f = lambda a: np.ascontiguousarray(np.asarray(a, dtype=np.float32))
    x = f(inputs['x']); c = f(inputs['c'])
    shared = {
        'ada_w': f(inputs['ada_w']).reshape(D, 6 * D),
        'ada_b': f(inputs['ada_b']).reshape(6 * D),
        'norm1_g': f(inputs['norm1_g']).reshape(D),
        'w_in': f(inputs['w_in']).reshape(D, NIN),
        'gm_vnorm_g': f(inputs['gm_vnorm_g']).reshape(512),
        'gm_spatial_w': f(inputs['gm_spatial_w']).reshape(4, 128, 128),
        'gm_spatial_b': f(inputs['gm_spatial_b']).reshape(512),
        'ml_conv_w': f(inputs['ml_conv_w']).reshape(4, 1024),
        'ml_conv_b': f(inputs['ml_conv_b']).reshape(1024),
        'ml_i_b': f(inputs['ml_i_b']).reshape(4, 1),
        'ml_f_b': f(inputs['ml_f_b']).reshape(4, 1),
        'ml_hnorm_g': f(inputs['ml_hnorm_g']).reshape(512),
        'w_out': f(inputs['w_out']).reshape(D, D),
        'norm2_g': f(inputs['norm2_g']).reshape(D),
        'ffn_w_up': f(inputs['ffn_w_up']).reshape(D, 2 * DFF),
        'ffn_conv_w': f(inputs['ffn_conv_w']).reshape(3, DFF),
        'ffn_conv_b': f(inputs['ffn_conv_b']).reshape(DFF),
        'ffn_w_down': f(inputs['ffn_w_down']).reshape(DFF, D),
        'final_ada_w': f(inputs['final_ada_w']).reshape(D, 2 * D),
        'final_ada_b': f(inputs['final_ada_b']).reshape(2 * D),
        'final_g': f(inputs['final_g']).reshape(D),
    }
    in_maps = []
    for i in range(NCORE):
        m = dict(shared)
        m['x'] = x[i * NSEQ:(i + 1) * NSEQ].reshape(NSEQ * S, D)
        m['c'] = c[i * NSEQ:(i + 1) * NSEQ]
        in_maps.append(m)
    res = run_bass_kernel_spmd(nc, in_maps, core_ids=list(range(NCORE)))
    out = np.concatenate([r['out'].reshape(NSEQ, S, D) for r in res.results], axis=0)
    return out.astype(np.float32)
```
